# Optimizing a Trainium2 kernel written in Bass

```python
import math
import jax, jax.numpy as jnp
from jax import lax
import numpy as np

D_MODEL = 2048
BATCH = 8
SEQ = 4096
DEPTH = 4

H_A = 8
DQK_A = 128
DV_A = 256
D_A = H_A * DV_A
QK_A = H_A * DQK_A
CHUNK = 64
CONV_K = 4
H_B = 16
D_NOPE = 128
D_ROPE = 64
D_HQK = D_NOPE + D_ROPE
D_VB = 128
D_B = H_B * D_VB
Q_LORA = 512
KV_LORA = 512
Q_BLOCK = 128
ROPE_THETA = 10000.0
EPS = 1e-6
NEG_INF = -1e30
SPLIT_SIZES = (QK_A, QK_A, D_A, D_A, H_A, H_A, D_A, Q_LORA, KV_LORA, D_ROPE, D_B, D_MODEL, D_MODEL)
N_IN = sum(SPLIT_SIZES)

kernel_name = "hybrid_mlstm_mla_gated_block"


def rms_norm(x, g):
    x32 = x.astype(jnp.float32)
    y = x32 * lax.rsqrt(jnp.mean(x32 * x32, axis=-1, keepdims=True) + EPS)
    return (y * g.astype(jnp.float32)).astype(x.dtype)


def causal_depthwise_conv(x, w):
    s_len = x.shape[1]
    xp = jnp.pad(x, ((0, 0), (CONV_K - 1, 0), (0, 0)))
    out = xp[:, CONV_K - 1:] * w[CONV_K - 1]
    for j in range(CONV_K - 1):
        out = out + xp[:, j:j + s_len] * w[j]
    return out


def rope_tables(positions):
    inv_freq = jnp.exp(-math.log(ROPE_THETA) * jnp.arange(0, D_ROPE, 2, dtype=jnp.float32) / D_ROPE)
    ang = positions.astype(jnp.float32)[..., None] * inv_freq
    return jnp.cos(ang)[:, :, None, :], jnp.sin(ang)[:, :, None, :]


def apply_rope(x, cos, sin):
    x32 = x.astype(jnp.float32)
    x1, x2 = jnp.split(x32, 2, axis=-1)
    return jnp.concatenate([x1 * cos - x2 * sin, x2 * cos + x1 * sin], axis=-1).astype(x.dtype)


def mlstm_chunkwise(q, k, v, i_pre, f_pre):
    b_sz, s_len, n_h, dk = q.shape
    dv = v.shape[-1]
    n_chunks = s_len // CHUNK
    f32 = jnp.float32

    def chunks(t):
        t = t.astype(f32).reshape((b_sz, n_chunks, CHUNK, n_h) + t.shape[3:])
        return jnp.moveaxis(t, (1, 3), (0, 2))

    qc_all = chunks(q) * (dk ** -0.5)
    kc_all = chunks(k)
    vc_all = chunks(v)
    ic_all = chunks(i_pre)
    lf_all = jax.nn.log_sigmoid(chunks(f_pre))
    causal = jnp.tril(jnp.ones((CHUNK, CHUNK), dtype=bool))

    def step(carry, xs):
        c_st, n_st, m_st = carry
        qc, kc, vc, ic, fc = xs
        b = jnp.cumsum(fc, axis=-1)
        g = b[..., -1]
        d = jnp.where(causal, b[..., :, None] - b[..., None, :] + ic[..., None, :], NEG_INF)
        inter = b + m_st[..., None]
        m_i = jnp.maximum(jnp.max(d, axis=-1), inter)
        s = jnp.einsum('bhid,bhjd->bhij', qc, kc) * jnp.exp(d - m_i[..., None])
        decay = jnp.exp(inter - m_i)
        num = jnp.einsum('bhij,bhje->bhie', s, vc) + decay[..., None] * jnp.einsum('bhid,bhde->bhie', qc, c_st)
        den = jnp.sum(s, axis=-1) + decay * jnp.einsum('bhid,bhd->bhi', qc, n_st)
        h = num / jnp.maximum(jnp.abs(den), jnp.exp(-m_i))[..., None]
        w = g[..., None] - b + ic
        m_new = jnp.maximum(g + m_st, jnp.max(w, axis=-1))
        carry_scale = jnp.exp(g + m_st - m_new)
        kw = kc * jnp.exp(w - m_new[..., None])[..., None]
        c_new = carry_scale[..., None, None] * c_st + jnp.einsum('bhjd,bhje->bhde', kw, vc)
        n_new = carry_scale[..., None] * n_st + jnp.sum(kw, axis=-2)
        return (c_new, n_new, m_new), h

    init = (jnp.zeros((b_sz, n_h, dk, dv), f32), jnp.zeros((b_sz, n_h, dk), f32), jnp.zeros((b_sz, n_h), f32))
    _, h_all = lax.scan(step, init, (qc_all, kc_all, vc_all, ic_all, lf_all))
    h_all = jnp.moveaxis(h_all, (0, 2), (1, 3)).reshape(b_sz, s_len, n_h, dv)
    return h_all.astype(v.dtype)


def causal_attention(q, k, v):
    b_sz, s_len, n_h, dq = q.shape
    n_blocks = s_len // Q_BLOCK
    scale = dq ** -0.5
    key_idx = jnp.arange(s_len)
    q_blocks = jnp.moveaxis(q.reshape(b_sz, n_blocks, Q_BLOCK, n_h, dq), 1, 0)

    def one_block(args):
        q_blk, blk_id = args
        q_idx = blk_id * Q_BLOCK + jnp.arange(Q_BLOCK)
        s = jnp.einsum('bqhd,bkhd->bhqk', q_blk, k).astype(jnp.float32) * scale
        s = jnp.where(key_idx[None, :] <= q_idx[:, None], s, NEG_INF)
        p = jax.nn.softmax(s, axis=-1).astype(v.dtype)
        return jnp.einsum('bhqk,bkhd->bqhd', p, v)

    o = lax.map(one_block, (q_blocks, jnp.arange(n_blocks)))
    return jnp.moveaxis(o, 0, 1).reshape(b_sz, s_len, n_h, v.shape[-1])


def hybrid_layer(x, cos, sin, norm_g, w_in, gate_bias, conv_w, mlstm_norm_g, w_a,
                 q_lat_g, kv_lat_g, w_uq, w_ukv, q_norm_g, k_norm_g, w_b, w_out):
    b_sz, s_len, _ = x.shape
    h = rms_norm(x, norm_g)
    proj = jnp.einsum('bsd,dn->bsn', h, w_in)
    offsets = [int(o) for o in np.cumsum(SPLIT_SIZES)[:-1]]
    (q_a, k_a, v_a, o_a, i_a, f_a, z_a, c_q, c_kv, k_r, z_b, g_a, g_b) = jnp.split(proj, offsets, axis=-1)

    qk_a = jax.nn.silu(causal_depthwise_conv(jnp.concatenate([q_a, k_a], axis=-1), conv_w))
    q_a, k_a = jnp.split(qk_a, 2, axis=-1)
    h_a = mlstm_chunkwise(q_a.reshape(b_sz, s_len, H_A, DQK_A),
                          k_a.reshape(b_sz, s_len, H_A, DQK_A),
                          v_a.reshape(b_sz, s_len, H_A, DV_A),
                          i_a + gate_bias[:H_A], f_a + gate_bias[H_A:])
    h_a = rms_norm(h_a, mlstm_norm_g.reshape(H_A, DV_A)).reshape(b_sz, s_len, D_A)
    y_a = jnp.einsum('bsc,cd->bsd', jax.nn.sigmoid(o_a) * h_a * jax.nn.silu(z_a), w_a)

    q_b = jnp.einsum('bsr,rn->bsn', rms_norm(c_q, q_lat_g), w_uq).reshape(b_sz, s_len, H_B, D_HQK)
    kv_b = jnp.einsum('bsr,rn->bsn', rms_norm(c_kv, kv_lat_g), w_ukv).reshape(b_sz, s_len, H_B, D_NOPE + D_VB)
    k_nope, v_b = jnp.split(kv_b, [D_NOPE], axis=-1)
    k_b = jnp.concatenate([k_nope, jnp.broadcast_to(k_r[:, :, None, :], (b_sz, s_len, H_B, D_ROPE))], axis=-1)
    q_b = rms_norm(q_b, q_norm_g)
    k_b = rms_norm(k_b, k_norm_g)
    q_b = jnp.concatenate([q_b[..., :D_NOPE], apply_rope(q_b[..., D_NOPE:], cos, sin)], axis=-1)
    k_b = jnp.concatenate([k_b[..., :D_NOPE], apply_rope(k_b[..., D_NOPE:], cos, sin)], axis=-1)
    o_b = causal_attention(q_b, k_b, v_b).reshape(b_sz, s_len, D_B)
    y_b = jnp.einsum('bsc,cd->bsd', o_b * jax.nn.silu(z_b), w_b)

    y = jax.nn.sigmoid(g_a) * y_a + jax.nn.sigmoid(g_b) * y_b
    return x + jnp.einsum('bsd,de->bse', y, w_out)


def setup_inputs(seed: int = 0) -> dict:
    key = jax.random.key(seed)
    ks = jax.random.split(key, 16)
    f32 = jnp.float32

    def nrm(k, shape, scale):
        return jax.random.normal(k, shape, f32) * scale

    x = nrm(ks[0], (BATCH, SEQ, D_MODEL), 1.0)
    positions = (jax.random.randint(ks[1], (BATCH, 1), 0, 1024, dtype=jnp.int32)
                 + jnp.arange(SEQ, dtype=jnp.int32)[None, :])
    norm_g = 1.0 + nrm(ks[2], (DEPTH, D_MODEL), 0.02)
    w_in = nrm(ks[3], (DEPTH, D_MODEL, N_IN), D_MODEL ** -0.5)
    kb1, kb2 = jax.random.split(ks[4])
    gate_bias = jnp.concatenate([nrm(kb1, (DEPTH, H_A), 0.1),
                                 jnp.linspace(3.0, 6.0, H_A, dtype=f32)[None, :] + nrm(kb2, (DEPTH, H_A), 0.1)], axis=-1)
    conv_w = nrm(ks[5], (DEPTH, CONV_K, 2 * QK_A), CONV_K ** -0.5)
    mlstm_norm_g = 1.0 + nrm(ks[6], (DEPTH, D_A), 0.02)
    w_a = nrm(ks[7], (DEPTH, D_A, D_MODEL), D_A ** -0.5)
    q_lat_g = 1.0 + nrm(ks[8], (DEPTH, Q_LORA), 0.02)
    kv_lat_g = 1.0 + nrm(ks[9], (DEPTH, KV_LORA), 0.02)
    w_uq = nrm(ks[10], (DEPTH, Q_LORA, H_B * D_HQK), Q_LORA ** -0.5)
    w_ukv = nrm(ks[11], (DEPTH, KV_LORA, H_B * (D_NOPE + D_VB)), KV_LORA ** -0.5)
    q_norm_g = 1.0 + nrm(ks[12], (DEPTH, D_HQK), 0.02)
    k_norm_g = 1.0 + nrm(ks[13], (DEPTH, D_HQK), 0.02)
    w_b = nrm(ks[14], (DEPTH, D_B, D_MODEL), D_B ** -0.5)
    w_out = nrm(ks[15], (DEPTH, D_MODEL, D_MODEL), D_MODEL ** -0.5)
    return {"x": x, "positions": positions, "norm_g": norm_g, "w_in": w_in, "gate_bias": gate_bias,
            "conv_w": conv_w, "mlstm_norm_g": mlstm_norm_g, "w_a": w_a, "q_lat_g": q_lat_g,
            "kv_lat_g": kv_lat_g, "w_uq": w_uq, "w_ukv": w_ukv, "q_norm_g": q_norm_g,
            "k_norm_g": k_norm_g, "w_b": w_b, "w_out": w_out}


def reference(x, positions, norm_g, w_in, gate_bias, conv_w, mlstm_norm_g, w_a, q_lat_g,
              kv_lat_g, w_uq, w_ukv, q_norm_g, k_norm_g, w_b, w_out):
    cos, sin = rope_tables(positions)
    for l in range(DEPTH):
        x = hybrid_layer(x, cos, sin, norm_g[l], w_in[l], gate_bias[l], conv_w[l], mlstm_norm_g[l],
                         w_a[l], q_lat_g[l], kv_lat_g[l], w_uq[l], w_ukv[l], q_norm_g[l],
                         k_norm_g[l], w_b[l], w_out[l])
    return x
```

```python
import numpy as np
import math
from contextlib import ExitStack
import concourse.bass as bass
import concourse.mybir as mybir
from concourse.bass_utils import run_bass_kernel_spmd

F32 = mybir.dt.float32
BF16 = mybir.dt.bfloat16
I32 = mybir.dt.int32
AF = mybir.ActivationFunctionType
ALU = mybir.AluOpType
AX = mybir.AxisListType


class Buf:
    __slots__ = ("w", "r", "w0", "name")

    def __init__(self, name=""):
        self.w = {}
        self.r = {}
        self.w0 = {}
        self.name = name


class T:
    __slots__ = ("ap", "buf")

    def __init__(self, ap, buf):
        self.ap = ap
        self.buf = buf

    def __getitem__(self, k):
        return T(self.ap[k], self.buf)

    def v(self, fn):
        return T(fn(self.ap), self.buf)


class Eng:
    def __init__(self, name, sem_id, dma_sem_ids):
        self.name = name
        self.sem = sem_id
        self.cnt = 0
        self.prog = []
        self.waited = {}
        self.dma_sems = dma_sem_ids
        self.dma_i = 0
        self.dma_val = {s: 0 for s in dma_sem_ids}
        self.pend = []
        self.ninstr = 0


class FW:
    NDMA = 8

    def __init__(self, nc, stack, arena_bytes):
        self.nc = nc
        self.sems = []
        def newsem(name):
            h = stack.enter_context(nc.semaphore(name))
            self.sems.append(h)
            return len(self.sems) - 1
        self.PE = Eng("PE", newsem("s_pe"), [])
        self.ACT = Eng("ACT", newsem("s_act"), [newsem(f"s_actq{i}") for i in range(self.NDMA)])
        self.DVE = Eng("DVE", newsem("s_dve"), [])
        self.POOL = Eng("POOL", newsem("s_pool"), [newsem(f"s_poolq{i}") for i in range(self.NDMA)])
        self.SP = Eng("SP", newsem("s_sp"), [newsem(f"s_spq{i}") for i in range(self.NDMA)])
        self.engs = [self.PE, self.ACT, self.DVE, self.POOL, self.SP]
        self.arena = stack.enter_context(nc.sbuf_tensor("arena", [128, arena_bytes // 4], F32))
        self.arena_bytes = arena_bytes
        self.top = 0
        self.banks = []
        for i in range(8):
            p = stack.enter_context(nc.psum_tensor(f"psb{i}", [128, 512], F32))
            self.banks.append(T(p[:, :], Buf(f"psb{i}")))

    def alloc(self, name, cols, dtype, parts=128):
        esz = 2 if dtype == BF16 else 4
        nbytes = (cols * esz + 31) // 32 * 32
        assert self.top + nbytes <= self.arena_bytes, f"arena overflow at {name}: {self.top}+{nbytes}"
        a = self.arena[0:parts, self.top // 4:(self.top + nbytes) // 4]
        if dtype != F32:
            a = a.bitcast(dtype)
        a = a[:, 0:cols]
        self.top += nbytes
        return T(a, Buf(name))

    def mark(self):
        return self.top

    def release(self, m):
        self.top = m

    def dram(self, name, shape, dtype, kind="Internal"):
        t = self.nc.dram_tensor(name, list(shape), dtype, kind=kind)
        return T(t.ap(), Buf(name))

    def op(self, E, fn, reads=(), writes=(), adds=(), sig=True, dma=False):
        waits = {}

        def need(d, skip_own=False):
            for sem, val in d.items():
                if skip_own and sem == E.sem:
                    continue
                if E.waited.get(sem, 0) < val:
                    if waits.get(sem, 0) < val:
                        waits[sem] = val

        pe = E is self.PE
        if not pe and self.PE.pend:
            for (prs, pws, pas) in self.PE.pend:
                for b in list(writes) + list(adds):
                    assert all(b is not x for x in prs), f"pending PE read hazard on {b.name}"
        for b in reads:
            need(b.w, skip_own=pe)
        for b in writes:
            need(b.w, skip_own=pe)
            need(b.r, skip_own=True)
        for b in adds:
            need(b.r, skip_own=True)
            need(b.w0, skip_own=pe)
        tok = None
        if dma:
            sem = E.dma_sems[E.dma_i]
            E.dma_i = (E.dma_i + 1) % len(E.dma_sems)
            prev = E.dma_val[sem]
            if prev > 0:
                need({sem: prev})
            E.dma_val[sem] = prev + 16
            tok = (sem, prev + 16)
        elif sig:
            E.cnt += 1
            tok = (E.sem, E.cnt)
        for sem, val in waits.items():
            E.waited[sem] = val
        wl = [(self.sems[s], v) for s, v in waits.items()]
        if dma:
            semh = self.sems[tok[0]]

            def run(e, wl=wl, fn=fn, semh=semh):
                for h, v in wl:
                    e.wait_ge(h, v)
                fn(e).then_inc(semh, 16)
        elif sig:
            semh = self.sems[E.sem]

            def run(e, wl=wl, fn=fn, semh=semh):
                for h, v in wl:
                    e.wait_ge(h, v)
                fn(e).then_inc(semh, 1)
        else:
            def run(e, wl=wl, fn=fn):
                for h, v in wl:
                    e.wait_ge(h, v)
                fn(e)
        E.prog.append(run)
        E.ninstr += 1
        if tok is None:
            E.pend.append((reads, writes, adds))
            return
        groups = [(reads, writes, adds)]
        if not dma and E.pend:
            groups += E.pend
            E.pend = []
        s, v = tok
        for (rs, ws, ads) in groups:
            for b in rs:
                if b.r.get(s, 0) < v:
                    b.r[s] = v
            for b in ws:
                b.w = {s: v}
                b.w0 = {s: v}
                b.r = {}
            for b in ads:
                if b.w.get(s, 0) < v:
                    b.w[s] = v

    def barrier(self):
        allt = {}
        for E in self.engs:
            if E.cnt > 0:
                allt[E.sem] = E.cnt
            for s, v in E.dma_val.items():
                if v > 0:
                    allt[s] = v
        for E in self.engs:
            assert not E.pend
            waits = {}
            for s, v in allt.items():
                if s == E.sem:
                    continue
                if E.waited.get(s, 0) < v:
                    waits[s] = v
                    E.waited[s] = v
            wl = [(self.sems[s], v) for s, v in waits.items()]
            if wl:
                def run(e, wl=wl):
                    for h, v in wl:
                        e.wait_ge(h, v)
                E.prog.append(run)

    def finish(self):
        self.barrier()
        nc = self.nc
        with nc.Block() as block:
            @block.tensor
            def _(e):
                for f in self.PE.prog:
                    f(e)

            @block.scalar
            def _(e):
                for f in self.ACT.prog:
                    f(e)

            @block.vector
            def _(e):
                for f in self.DVE.prog:
                    f(e)

            @block.gpsimd
            def _(e):
                for f in self.POOL.prog:
                    f(e)

            @block.sync
            def _(e):
                for f in self.SP.prog:
                    f(e)

    def mm(self, out, lhsT, rhs, start=True, stop=True, excl=None):
        o, l, r = out.ap, lhsT.ap, rhs.ap
        fn = lambda e: e.matmul(o, l, r, start=start, stop=stop)
        if excl is None:
            excl = start
        if excl:
            self.op(self.PE, fn, reads=(lhsT.buf, rhs.buf), writes=(out.buf,), sig=stop)
        else:
            self.op(self.PE, fn, reads=(lhsT.buf, rhs.buf), adds=(out.buf,), sig=stop)

    def transpose(self, out, in_, ident):
        o, i, d = out.ap, in_.ap, ident.ap
        self.op(self.PE, lambda e: e.transpose(o, i, d), reads=(in_.buf, ident.buf), writes=(out.buf,))

    def transpose_add(self, out, in_, ident, sig=True):
        o, i, d = out.ap, in_.ap, ident.ap
        self.op(self.PE, lambda e: e.transpose(o, i, d), reads=(in_.buf, ident.buf), adds=(out.buf,), sig=sig)

    def act(self, out, in_, func, bias=None, scale=None, accum=None, add=False, accum_add=False):
        o, i = out.ap, in_.ap
        kw = {}
        reads = [in_.buf]
        writes = [out.buf]
        if bias is not None:
            if isinstance(bias, T):
                kw["bias"] = bias.ap
                reads.append(bias.buf)
            else:
                kw["bias"] = bias
        if scale is not None:
            if isinstance(scale, T):
                kw["scale"] = scale.ap
                reads.append(scale.buf)
            else:
                kw["scale"] = scale
        adds = []
        if accum is not None:
            kw["accum_out"] = accum.ap
            (adds if accum_add else writes).append(accum.buf)
        if add:
            self.op(self.ACT, lambda e: e.activation(o, i, func, **kw), reads=reads, adds=writes + adds)
        else:
            self.op(self.ACT, lambda e: e.activation(o, i, func, **kw), reads=reads, writes=writes, adds=adds)

    def _e(self, E):
        return E

    def tt(self, E, out, in0, in1, op, add=False):
        o, a, b = out.ap, in0.ap, in1.ap
        fn = lambda e: e.tensor_tensor(out=o, in0=a, in1=b, op=op)
        if add:
            self.op(E, fn, reads=(in0.buf, in1.buf), adds=(out.buf,))
        else:
            self.op(E, fn, reads=(in0.buf, in1.buf), writes=(out.buf,))

    def ts(self, E, out, in0, s1, s2=None, op0=ALU.mult, op1=None, add=False):
        o, a = out.ap, in0.ap
        reads = [in0.buf]
        if isinstance(s1, T):
            reads.append(s1.buf)
            s1 = s1.ap
        if isinstance(s2, T):
            reads.append(s2.buf)
            s2 = s2.ap
        if op1 is None:
            fn = lambda e: e.tensor_scalar(out=o, in0=a, scalar1=s1, scalar2=None, op0=op0)
        else:
            fn = lambda e: e.tensor_scalar(out=o, in0=a, scalar1=s1, scalar2=s2, op0=op0, op1=op1)
        if add:
            self.op(E, fn, reads=reads, adds=(out.buf,))
        else:
            self.op(E, fn, reads=reads, writes=(out.buf,))

    def stt(self, out, in0, scalar, in1, op0, op1, add=False, deps=()):
        o, a, b = out.ap, in0.ap, in1.ap
        reads = [in0.buf, in1.buf] + list(deps)
        if isinstance(scalar, T):
            reads.append(scalar.buf)
            scalar = scalar.ap
        fn = lambda e: e.scalar_tensor_tensor(out=o, in0=a, scalar=scalar, in1=b, op0=op0, op1=op1)
        if add:
            self.op(self.DVE, fn, reads=reads, adds=(out.buf,))
        else:
            self.op(self.DVE, fn, reads=reads, writes=(out.buf,))

    def copy(self, E, out, in_, add=False):
        o, i = out.ap, in_.ap
        if E is self.ACT:
            fn = lambda e: e.copy(o, i)
        else:
            fn = lambda e: e.tensor_copy(out=o, in_=i)
        if add:
            self.op(E, fn, reads=(in_.buf,), adds=(out.buf,))
        else:
            self.op(E, fn, reads=(in_.buf,), writes=(out.buf,))

    def recip(self, out, in_):
        o, i = out.ap, in_.ap
        self.op(self.DVE, lambda e: e.reciprocal(o, i), reads=(in_.buf,), writes=(out.buf,))

    def memset(self, E, out, val, add=False):
        o = out.ap
        if add:
            self.op(E, lambda e: e.memset(o, val), adds=(out.buf,))
        else:
            self.op(E, lambda e: e.memset(o, val), writes=(out.buf,))

    def dma(self, E, out, in_, add=False, **kw):
        o, i = out.ap, in_.ap
        fn = lambda e: e.dma_start(out=o, in_=i, **kw)
        if add:
            self.op(E, fn, reads=(in_.buf,), adds=(out.buf,), dma=True)
        else:
            self.op(E, fn, reads=(in_.buf,), writes=(out.buf,), dma=True)


D = 2048
NIN = 15440
O_QA, O_KA, O_VA, O_OA, O_IA, O_FA, O_ZA = 0, 1024, 2048, 4096, 6144, 6152, 6160
O_CQ, O_CKV, O_KR, O_ZB, O_GA, O_GB = 8208, 8720, 9232, 9296, 11344, 13392
EPS = 1e-6
LN_DK = math.log(128.0 ** -0.5)
ATT_SCALE = 192.0 ** -0.5
TWO_PI_LO = 6.2831845
W_NAMES = ["norm_g", "w_in", "gate_bias", "conv_w", "mlstm_norm_g", "w_a", "q_lat_g", "kv_lat_g",
           "w_uq", "w_ukv", "q_norm_g", "k_norm_g", "w_b", "w_out"]
W_SHAPES = {"norm_g": [D], "w_in": [D, NIN], "gate_bias": [16], "conv_w": [4, 2048], "mlstm_norm_g": [2048],
            "w_a": [2048, D], "q_lat_g": [512], "kv_lat_g": [512], "w_uq": [512, 3072], "w_ukv": [512, 4096],
            "q_norm_g": [192], "k_norm_g": [192], "w_b": [2048, D], "w_out": [D, D]}


def r3(_x, pat, **kw):
    return _x.v(lambda a: a.rearrange(pat, **kw))


class WStream:
    def __init__(self, fw, jobs, nst=2, nbf=4, width=256):
        self.fw = fw
        self.jobs = jobs
        self.st = [fw.alloc(f"wst{i}", 16 * width, F32) for i in range(nst)]
        self.bf = [fw.alloc(f"wbf{i}", 16 * width, BF16) for i in range(nbf)]
        self.issued = 0
        self.taken = 0
        self.tiles = {}

    def _issue(self):
        i = self.issued
        if i >= len(self.jobs):
            return
        src, col0, n = self.jobs[i]
        fw = self.fw
        ws = r3(self.st[i % len(self.st)][:, 0:16 * n], "p (c n) -> p c n", c=16)
        wb = r3(self.bf[i % len(self.bf)][:, 0:16 * n], "p (c n) -> p c n", c=16)
        fw.dma(fw.SP, ws, src[:, col0:col0 + n].v(lambda a: a.rearrange("(c p) n -> p c n", p=128)))
        fw.copy(fw.POOL, wb, ws)
        self.tiles[i] = wb
        self.issued += 1

    def get(self, ahead=1):
        while self.issued <= self.taken + ahead and self.issued < len(self.jobs):
            self._issue()
        t = self.tiles.pop(self.taken)
        self.taken += 1
        return t


def build_program(S, NL, debug=False, stop=None):
    nc = bass.Bass("TRN2", target_bir_lowering=False)
    st = ExitStack()
    fw = FW(nc, st, 188 * 1024)
    PE, ACT, DVE, POOL, SP = fw.PE, fw.ACT, fw.DVE, fw.POOL, fw.SP
    STQ = SP
    banks = fw.banks
    NT = S // 128
    TB = min(S, 2048)
    NTB = S // TB
    NTT = TB // 128
    NSB = TB // 512
    NB5 = S // 512
    skind = "ExternalOutput" if debug else "Internal"

    x_in = fw.dram("x", [S, D], F32, kind="ExternalInput")
    pos = fw.dram("pos", [1, S], I32, kind="ExternalInput")
    invf = fw.dram("invf", [64, 1], F32, kind="ExternalInput")
    Wd = {n: fw.dram(n, [NL] + W_SHAPES[n], F32, kind="ExternalInput") for n in W_NAMES}
    out_d = fw.dram("out", [S, D], F32, kind="ExternalOutput")
    xmid = [fw.dram(f"xmid{i}", [S, D], F32) for i in range(max(NL - 1, 0))]

    cos_sc = fw.dram("cos_sc", [64, S], F32, kind=skind)
    sin_sc = fw.dram("sin_sc", [64, S], F32, kind=skind)
    qkT_sc = fw.dram("qkT_sc", [2048, S], BF16, kind=skind)
    v_sc = fw.dram("v_sc", [S, 2048], BF16, kind=skind)
    oz_sc = fw.dram("oz_sc", [S, 2048], BF16, kind=skind)
    gates_sc = fw.dram("gates_sc", [S, 16], F32, kind=skind)
    cqT_sc = fw.dram("cqT_sc", [512, S], BF16, kind=skind)
    ckvT_sc = fw.dram("ckvT_sc", [512, S], BF16, kind=skind)
    kr_sc = fw.dram("kr_sc", [64, S], F32, kind=skind)
    krot_sc = fw.dram("krot_sc", [64, S], F32, kind=skind)
    szbT_sc = fw.dram("szbT_sc", [2048, S], BF16, kind=skind)
    sgaT_sc = fw.dram("sgaT_sc", [2048, S], BF16, kind=skind)
    sgbT_sc = fw.dram("sgbT_sc", [2048, S], BF16, kind=skind)
    uaT_sc = fw.dram("uaT_sc", [2048, S], BF16, kind=skind)
    QT_sc = fw.dram("QT_sc", [16, 192, S], BF16, kind=skind)
    KT_sc = fw.dram("KT_sc", [16, 192, S], BF16, kind=skind)
    V_sc = fw.dram("V_sc", [S, 2048], BF16, kind=skind)
    ubT_sc = fw.dram("ubT_sc", [2048, S], BF16, kind=skind)

    identf = fw.alloc("identf", 128, F32)
    ident = fw.alloc("ident", 128, BF16)
    trif = fw.alloc("trif", 128, F32)
    trib = fw.alloc("trib", 128, BF16)
    onesf = fw.alloc("onesf", 128, F32)
    onesb = fw.alloc("onesb", 128, BF16)
    fw.memset(POOL, identf, 0.0)
    ia = identf.ap
    fw.op(POOL, lambda e: e.affine_select(out=ia, in_=ia, pattern=[[-1, 128]], compare_op=ALU.not_equal,
                                           fill=1.0, base=0, channel_multiplier=1),
          reads=(identf.buf,), writes=(identf.buf,))
    fw.copy(DVE, ident, identf)
    fw.memset(POOL, onesf, 1.0)
    fw.copy(DVE, onesb, onesf)
    ta, oa = trif.ap, onesf.ap
    fw.op(POOL, lambda e: e.affine_select(out=ta, in_=oa, pattern=[[1, 128]], compare_op=ALU.is_ge,
                                           fill=0.0, base=0, channel_multiplier=-1),
          reads=(onesf.buf,), writes=(trif.buf,))
    fw.copy(DVE, trib, trif)

    pctr = [0]

    def nbank():
        b = banks[pctr[0] % 8]
        pctr[0] += 1
        return b

    m0 = fw.mark()
    CB = min(S, 1024)
    invt = fw.alloc("invt", 1, F32, parts=64)
    fw.dma(SP, invt, invf)
    posi = fw.alloc("posi", CB, I32, parts=64)
    ang = fw.alloc("ang", CB, F32, parts=64)
    tq = fw.alloc("tq", CB, F32, parts=64)
    ki = fw.alloc("ki", CB, I32, parts=64)
    kf = fw.alloc("kf", CB, F32, parts=64)
    rr_ = fw.alloc("rr_", CB, F32, parts=64)
    mk = fw.alloc("mk", CB, F32, parts=64)
    res = [fw.alloc(f"res{i}", CB, F32, parts=64) for i in range(2)]
    k = 0
    for cb in range(S // CB):
        sl = slice(cb * CB, (cb + 1) * CB)
        fw.dma(SP, posi, pos[:, sl].v(lambda a: a.partition_broadcast(64).rearrange("p o s -> p (o s)")))
        fw.copy(DVE, ang, posi)
        fw.ts(DVE, ang, ang, invt[:, 0:1], None, op0=ALU.mult)
        for (off, dst) in ((0.0, sin_sc), (0.25, cos_sc)):
            fw.ts(DVE, tq, ang, 1.0 / (2.0 * math.pi), off, op0=ALU.mult, op1=ALU.add)
            fw.copy(DVE, ki, tq)
            fw.copy(DVE, kf, ki)
            fw.tt(DVE, rr_, tq, kf, ALU.subtract)
            fw.ts(DVE, mk, rr_, 0.5, None, op0=ALU.is_gt)
            fw.tt(DVE, rr_, rr_, mk, ALU.subtract)
            fw.ts(DVE, mk, rr_, -0.5, None, op0=ALU.is_lt)
            fw.tt(DVE, rr_, rr_, mk, ALU.add)
            rs_ = res[k % 2]
            k += 1
            fw.act(rs_, rr_, AF.Sin, scale=TWO_PI_LO)
            fw.dma(SP, dst[:, sl], rs_, add=True)
    fw.barrier()
    fw.release(m0)

    for l in range(NL):
        if stop == 'R':
            break
        x_src = x_in if l == 0 else xmid[l - 1]
        x_dst = out_d if l == NL - 1 else xmid[l]
        Wl = {n: Wd[n][l] for n in W_NAMES}
        w_in = Wl["w_in"]
        lay_mark = fw.mark()

        g_bc = fw.alloc("g_bc", D, F32)
        fw.dma(SP, g_bc, Wl["norm_g"].v(lambda a: a.partition_broadcast(128)))
        convw = fw.alloc("convw", 64, F32)
        convw3 = r3(convw, "p (c j) -> p c j", c=16)
        for c in range(16):
            fw.dma(SP, convw3[:, c, :], Wl["conv_w"][:, c * 128:(c + 1) * 128].v(lambda a: a.rearrange("j p -> p j")),
                   add=(c > 0), allow_slow_non_contiguous=True)
        gmT = fw.alloc("gmT", 16, F32)
        for c4 in range(4):
            fw.dma(SP, gmT[:, c4 * 4:(c4 + 1) * 4],
                   Wl["mlstm_norm_g"][c4 * 512:(c4 + 1) * 512].v(lambda a: a.rearrange("(c p) -> p c", p=128)),
                   add=(c4 > 0), allow_slow_non_contiguous=True)
        glq = fw.alloc("glq", 4, F32)
        fw.dma(SP, glq, Wl["q_lat_g"].v(lambda a: a.rearrange("(c p) -> p c", p=128)), allow_slow_non_contiguous=True)
        glk = fw.alloc("glk", 4, F32)
        fw.dma(SP, glk, Wl["kv_lat_g"].v(lambda a: a.rearrange("(c p) -> p c", p=128)), allow_slow_non_contiguous=True)
        gn = {}
        colv = lambda a: a.rearrange("(p o) -> p o", o=1)
        for nm, src in (("q", Wl["q_norm_g"]), ("k", Wl["k_norm_g"])):
            t = fw.alloc("gn" + nm, 3, F32)
            fw.dma(SP, t[:, 0:1], src[0:128].v(colv), allow_slow_non_contiguous=True)
            fw.dma(SP, t[0:64, 1:2], src[128:192].v(colv), add=True, allow_slow_non_contiguous=True)
            fw.dma(SP, t[0:32, 2:3], src[160:192].v(colv), add=True, allow_slow_non_contiguous=True)
            fw.dma(SP, t[32:64, 2:3], src[128:160].v(colv), add=True, allow_slow_non_contiguous=True)
            gn[nm] = t
        gb_bc = fw.alloc("gb_bc", 16, F32)
        fw.dma(SP, gb_bc, Wl["gate_bias"].v(lambda a: a.partition_broadcast(128)))
        halo = fw.alloc("halo", 48, F32)
        halo3 = r3(halo, "p (c j) -> p c j", c=16)
        fw.memset(POOL, halo, 0.0)
        par_mark = fw.mark()
        if stop == 'P':
            break

        for tb in range(NTB):
            t0 = tb * TB
            fw.release(par_mark)
            hT = fw.alloc("hT", 16 * TB, BF16)
            hT3 = r3(hT, "p (c t) -> p c t", c=16)
            mA = fw.mark()
            xs = [fw.alloc(f"xs{i}", D, F32) for i in range(2)]
            xb = [fw.alloc(f"xb{i}", D, BF16) for i in range(2)]
            stat = [fw.alloc(f"stat{i}", 4, F32) for i in range(2)]
            for tt in range(NTT):
                xt, xbt, sv = xs[tt % 2], xb[tt % 2], stat[tt % 2]
                fw.dma(SP, xt, x_src[t0 + tt * 128:t0 + (tt + 1) * 128, :])
                fw.act(xbt, xt, AF.Square, accum=sv[:, 0:1])
                fw.ts(DVE, sv[:, 1:2], sv[:, 0:1], 1.0 / D, EPS, op0=ALU.mult, op1=ALU.add)
                fw.act(sv[:, 2:3], sv[:, 1:2], AF.Ln)
                fw.act(sv[:, 3:4], sv[:, 2:3], AF.Exp, scale=-0.5)
                fw.stt(xbt, xt, sv[:, 3:4], g_bc, ALU.mult, ALU.mult)
                for g in range(4):
                    pb = nbank().v(lambda a: a.bitcast(BF16))
                    for j in range(4):
                        c = g * 4 + j
                        fw.transpose_add(pb[:, j * 128:(j + 1) * 128], xbt[:, c * 128:(c + 1) * 128], ident, sig=(j == 3))
                    fw.copy(ACT if g % 2 == 0 else DVE, hT3[:, g * 4:(g + 1) * 4, tt * 128:(tt + 1) * 128],
                            r3(pb[:, 0:512], "p (c t) -> p c t", c=4), add=True)
            fw.barrier()
            fw.release(mA)
            if stop == 'A':
                break

            jobs = []
            for g in range(8):
                jobs.append((w_in, O_QA + g * 256, 256))
            for g in range(8):
                jobs.append((w_in, O_VA + g * 256, 256))
            for g in range(8):
                jobs.append((w_in, O_OA + g * 256, 256))
                jobs.append((w_in, O_ZA + g * 256, 256))
            jobs.append((w_in, O_IA, 16))
            jobs += [(w_in, O_CQ, 256), (w_in, O_CQ + 256, 256), (w_in, O_CKV, 256), (w_in, O_CKV + 256, 256)]
            jobs.append((w_in, O_KR, 64))
            for o in (O_ZB, O_GA, O_GB):
                for g in range(8):
                    jobs.append((w_in, o + g * 256, 256))
            wsx = WStream(fw, jobs)
            mB = fw.mark()

            raw = [fw.alloc(f"raw{i}", 3 + TB, F32) for i in range(2)]
            acc = [fw.alloc(f"acc{i}", TB, F32) for i in range(2)]
            ob = [fw.alloc(f"ob{i}", TB, BF16) for i in range(2)]
            for g in range(8):
                wb = wsx.get()
                for cc2 in range(2):
                    cc = g * 2 + cc2
                    rw, ac, o = raw[cc % 2], acc[cc % 2], ob[cc % 2]
                    fw.copy(POOL, rw[:, 0:3], halo3[:, cc, :])
                    for sb in range(NSB):
                        ps = nbank()
                        for c in range(16):
                            fw.mm(ps, wb[:, c, cc2 * 128:(cc2 + 1) * 128], hT3[:, c, sb * 512:(sb + 1) * 512],
                                  start=(c == 0), stop=(c == 15))
                        fw.copy(ACT, rw[:, 3 + sb * 512:3 + (sb + 1) * 512], ps, add=True)
                    fw.copy(POOL, halo3[:, cc, :], rw[:, TB:TB + 3], add=True)
                    fw.ts(DVE, ac, rw[:, 3:3 + TB], convw3[:, cc, 3:4], None, op0=ALU.mult)
                    for j in range(3):
                        fw.stt(ac, rw[:, j:j + TB], convw3[:, cc, j:j + 1], ac, ALU.mult, ALU.add)
                    fw.act(o, ac, AF.Silu)
                    fw.dma(STQ, qkT_sc[cc * 128:(cc + 1) * 128, t0:t0 + TB], o, add=True)
            fw.barrier()
            fw.release(mB)

            if stop == 'B1':
                fw.barrier()
                break
            ost = [fw.alloc(f"ost{i}", NTT * 256, BF16) for i in range(2)]
            for g in range(8):
                wb = wsx.get()
                o3 = r3(ost[g % 2], "p (t n) -> p t n", t=NTT)
                for tt in range(NTT):
                    ps = nbank()
                    for c in range(16):
                        fw.mm(ps[:, 0:256], hT3[:, c, tt * 128:(tt + 1) * 128], wb[:, c, :], start=(c == 0), stop=(c == 15))
                    fw.copy(ACT if tt % 2 == 0 else DVE, o3[:, tt, :], ps[:, 0:256], add=(tt > 0))
                for tt in range(NTT):
                    fw.dma(STQ, v_sc[t0 + tt * 128:t0 + (tt + 1) * 128, g * 256:(g + 1) * 256], o3[:, tt, :], add=True)

            if stop == 'B2':
                fw.barrier()
                break
            fw.barrier()
            fw.release(mB)
            ost = [fw.alloc(f"ost{i}", NTT * 256, BF16) for i in range(2)]
            s1 = [fw.alloc(f"s1{i}", 256, F32) for i in range(2)]
            s2 = [fw.alloc(f"s2{i}", 256, F32) for i in range(2)]
            k = 0
            for g in range(8):
                wo = wsx.get(ahead=2)
                wz = wsx.get(ahead=1)
                o3 = r3(ost[g % 2], "p (t n) -> p t n", t=NTT)
                for tt in range(NTT):
                    pso, psz = nbank(), nbank()
                    for c in range(16):
                        fw.mm(pso[:, 0:256], hT3[:, c, tt * 128:(tt + 1) * 128], wo[:, c, :], start=(c == 0), stop=(c == 15))
                    for c in range(16):
                        fw.mm(psz[:, 0:256], hT3[:, c, tt * 128:(tt + 1) * 128], wz[:, c, :], start=(c == 0), stop=(c == 15))
                    a1, a2 = s1[k % 2], s2[k % 2]
                    k += 1
                    fw.act(a1, pso[:, 0:256], AF.Sigmoid)
                    fw.act(a2, psz[:, 0:256], AF.Silu)
                    fw.tt(DVE, o3[:, tt, :], a1, a2, ALU.mult, add=(tt > 0))
                for tt in range(NTT):
                    fw.dma(STQ, oz_sc[t0 + tt * 128:t0 + (tt + 1) * 128, g * 256:(g + 1) * 256], o3[:, tt, :], add=True)

            if stop == 'B3':
                fw.barrier()
                break
            fw.barrier()
            fw.release(mB)
            wg = wsx.get()
            gst = fw.alloc("gst", NTT * 16, F32)
            gst3 = r3(gst, "p (t n) -> p t n", t=NTT)
            gtm = fw.alloc("gtm", NTT * 8, F32)
            gtm3 = r3(gtm, "p (t n) -> p t n", t=NTT)
            for tt in range(NTT):
                ps = nbank()
                for c in range(16):
                    fw.mm(ps[:, 0:16], hT3[:, c, tt * 128:(tt + 1) * 128], wg[:, c, :], start=(c == 0), stop=(c == 15))
                fw.tt(DVE, gst3[:, tt, :], ps[:, 0:16], gb_bc, ALU.add, add=(tt > 0))
            fw.act(gtm3, gst3[:, :, 8:16], AF.Exp, scale=-1.0)
            fw.act(gtm3, gtm3, AF.Ln, bias=1.0)
            fw.ts(DVE, gst3[:, :, 8:16], gtm3, -1.0, None, op0=ALU.mult, add=True)
            for tt in range(NTT):
                fw.dma(STQ, gates_sc[t0 + tt * 128:t0 + (tt + 1) * 128, :], gst3[:, tt, :], add=True)

            if stop == 'B4':
                fw.barrier()
                break
            fw.barrier()
            fw.release(mB)
            lraw = fw.alloc("lraw", 4 * 512, F32)
            lraw3 = r3(lraw, "p (c t) -> p c t", c=4)
            lsq = fw.alloc("lsq", 4 * 512, BF16)
            lsq3 = r3(lsq, "p (c t) -> p c t", c=4)
            ltmp = fw.alloc("ltmp", 512, F32)
            lrs = fw.alloc("lrs", 512, F32)
            lout = [fw.alloc(f"lout{i}", 4 * 512, BF16) for i in range(2)]
            k = 0
            for (gl, dst) in ((glq, cqT_sc), (glk, ckvT_sc)):
                wA = wsx.get(ahead=2)
                wB = wsx.get(ahead=1)
                for sb in range(NSB):
                    for c4 in range(4):
                        wb = wA if c4 < 2 else wB
                        ps = nbank()
                        for c in range(16):
                            fw.mm(ps, wb[:, c, (c4 % 2) * 128:(c4 % 2 + 1) * 128], hT3[:, c, sb * 512:(sb + 1) * 512],
                                  start=(c == 0), stop=(c == 15))
                        fw.copy(DVE, lraw3[:, c4, :], ps, add=(c4 > 0))
                        fw.act(lsq3[:, c4, :], lraw3[:, c4, :], AF.Square, add=(c4 > 0))
                    pss = nbank()
                    for c4 in range(4):
                        fw.mm(pss, onesb, lsq3[:, c4, :], start=(c4 == 0), stop=(c4 == 3))
                    fw.ts(DVE, ltmp, pss, 1.0 / 512, EPS, op0=ALU.mult, op1=ALU.add)
                    fw.act(ltmp, ltmp, AF.Ln)
                    fw.act(lrs, ltmp, AF.Exp, scale=-0.5)
                    lo = lout[k % 2]
                    k += 1
                    lo3 = r3(lo, "p (c t) -> p c t", c=4)
                    for c4 in range(4):
                        fw.stt(lo3[:, c4, :], lraw3[:, c4, :], gl[:, c4:c4 + 1], lrs, ALU.mult, ALU.mult, add=(c4 > 0))
                    for c4 in range(4):
                        fw.dma(STQ, dst[c4 * 128:(c4 + 1) * 128, t0 + sb * 512:t0 + (sb + 1) * 512], lo3[:, c4, :], add=True)
            if stop == 'B5a':
                fw.barrier()
                break
            wk = wsx.get()
            wkr = fw.alloc("wkr", 16 * 64, BF16)
            wkr3 = r3(wkr, "p (c n) -> p c n", c=16)
            fw.ts(POOL, wkr3[:, :, 0:32], wk[:, :, 32:64], -1.0, None, op0=ALU.mult)
            fw.copy(POOL, wkr3[:, :, 32:64], wk[:, :, 0:32], add=True)
            if stop == 'B5b':
                fw.barrier()
                break
            kst = [fw.alloc(f"kst{i}", 512, F32, parts=64) for i in range(2)]
            k = 0
            for sb in range(NSB):
                for (wsel, dst) in ((wk, kr_sc), (wkr3, krot_sc)):
                    ps = nbank()
                    for c in range(16):
                        fw.mm(ps[0:64, :], wsel[:, c, :], hT3[:, c, sb * 512:(sb + 1) * 512], start=(c == 0), stop=(c == 15))
                    ko = kst[k % 2]
                    k += 1
                    fw.copy(ACT, ko, ps[0:64, :])
                    fw.dma(STQ, dst[:, t0 + sb * 512:t0 + (sb + 1) * 512], ko, add=True)

            if stop == 'B5':
                fw.barrier()
                break
            fw.barrier()
            fw.release(mB)
            ob2 = [fw.alloc(f"ob2{i}", TB, BF16) for i in range(2)]
            k = 0
            for (func, dst) in ((AF.Silu, szbT_sc), (AF.Sigmoid, sgaT_sc), (AF.Sigmoid, sgbT_sc)):
                for g in range(8):
                    wb = wsx.get()
                    for cc2 in range(2):
                        cc = g * 2 + cc2
                        o = ob2[k % 2]
                        k += 1
                        for sb in range(NSB):
                            ps = nbank()
                            for c in range(16):
                                fw.mm(ps, wb[:, c, cc2 * 128:(cc2 + 1) * 128], hT3[:, c, sb * 512:(sb + 1) * 512],
                                      start=(c == 0), stop=(c == 15))
                            fw.act(o[:, sb * 512:(sb + 1) * 512], ps, func, add=(sb > 0))
                        fw.dma(STQ, dst[cc * 128:(cc + 1) * 128, t0:t0 + TB], o, add=True)
            fw.barrier()
        fw.release(par_mark)

        if stop in ('A', 'B', 'B1', 'B2', 'B3', 'B4', 'B5', 'B5a', 'B5b'):
            break
        Cm = fw.alloc("Cm", 8 * 260, F32)
        Cm3 = r3(Cm, "p (h e) -> p h e", h=8)
        Cb3 = [r3(fw.alloc(f"Cb{i}", 8 * 260, BF16), "p (h e) -> p h e", h=8) for i in range(2)]
        fw.memset(POOL, Cm, 0.0)
        fw.memset(POOL, Cb3[0], 0.0)
        fw.memset(POOL, Cb3[1], 0.0)
        qk3s = [r3(fw.alloc(f"qk{i}", 16 * 512, BF16), "p (c t) -> p c t", c=16) for i in range(2)]
        va4s = [r3(fw.alloc(f"va{i}", 4 * 8 * 258, BF16), "p (t h e) -> p t h e", t=4, h=8) for i in range(2)]
        oz3s = [r3(fw.alloc(f"ozt{i}", 4 * 2048, BF16), "p (t n) -> p t n", t=4) for i in range(2)]
        g3s = [r3(fw.alloc(f"gt{i}", 4 * 16, F32), "p (t n) -> p t n", t=4) for i in range(2)]
        uT3s = [r3(fw.alloc(f"uT{i}", 16 * 512, BF16), "p (c t) -> p c t", c=16) for i in range(2)]
        for i in range(2):
            fw.memset(POOL, va4s[i][:, :, :, 256:257], 1.0)
        Et = [fw.alloc(f"Et{i}", 32, F32) for i in range(2)]
        EXt = [fw.alloc(f"EXt{i}", 32, F32) for i in range(2)]
        sTs = [fw.alloc(f"sT{i}", 128, BF16) for i in range(3)]
        kws = [fw.alloc(f"kw{i}", 128, BF16) for i in range(3)]
        hb3s = [r3(fw.alloc(f"hbuf{i}", 8 * 256, F32), "p (h e) -> p h e", h=8) for i in range(2)]
        junks = [fw.alloc(f"junk{i}", 256, BF16) for i in range(8)]
        sss = [fw.alloc(f"ss{i}", 8, F32) for i in range(2)]
        dts = [fw.alloc(f"dt{i}", 64, F32) for i in range(2)]
        us = [fw.alloc(f"u{i}", 2048, BF16) for i in range(2)]
        for sc in range(NB5):
            tok0 = sc * 512
            qk3, v4, oz3, g3, uT3 = qk3s[sc % 2], va4s[sc % 2], oz3s[sc % 2], g3s[sc % 2], uT3s[sc % 2]
            fw.dma(SP, qk3, qkT_sc[:, tok0:tok0 + 512].v(lambda a: a.rearrange("(c p) t -> p c t", p=128)))
            for t in range(4):
                fw.dma(SP, v4[:, t, :, 0:256],
                       v_sc[tok0 + t * 128:tok0 + (t + 1) * 128, :].v(lambda a: a.rearrange("p (h e) -> p h e", h=8)), add=True)
            fw.dma(SP, oz3, oz_sc[tok0:tok0 + 512, :].v(lambda a: a.rearrange("(t p) n -> p t n", p=128)))
            fw.dma(SP, g3, gates_sc[tok0:tok0 + 512, :].v(lambda a: a.rearrange("(t p) n -> p t n", p=128)))
            for t in range(4):
                ci = sc * 4 + t
                Cbc, Cbn = Cb3[ci % 2], Cb3[(ci + 1) % 2]
                ic, lf = g3[:, t, 0:8], g3[:, t, 8:16]
                gps = banks[0]
                fw.mm(gps[:, 0:8], trif, lf, True, True)
                fw.mm(gps[:, 8:16], onesf, lf, True, True, excl=False)
                E, EX = Et[ci % 2], EXt[ci % 2]
                fw.tt(DVE, E[:, 0:8], ic, gps[:, 0:8], ALU.subtract)
                fw.tt(DVE, E[:, 8:16], E[:, 0:8], gps[:, 8:16], ALU.add, add=True)
                fw.ts(DVE, E[:, 16:24], gps[:, 0:8], LN_DK, None, op0=ALU.add, add=True)
                fw.copy(DVE, E[:, 24:32], gps[:, 8:16], add=True)
                fw.act(EX, E, AF.Exp)
                hb3, ss, dt, u = hb3s[ci % 2], sss[ci % 2], dts[ci % 2], us[ci % 2]
                dps = banks[1]
                for h in range(8):
                    tsl = slice(t * 128, (t + 1) * 128)
                    qTh, kTh = qk3[:, h, tsl], qk3[:, 8 + h, tsl]
                    sps = banks[2 + (h % 2)]
                    fw.mm(sps[:, 0:128], kTh, qTh)
                    sT = sTs[h % 3]
                    fw.stt(sT, sps[:, 0:128], EX[:, h:h + 1], trib, ALU.mult, ALU.mult)
                    kps = banks[7].v(lambda a: a.bitcast(BF16))
                    fw.transpose(kps[:, 0:128], kTh, ident)
                    kw = kws[h % 3]
                    fw.act(kw, kps[:, 0:128], AF.Copy, scale=EX[:, 8 + h:9 + h])
                    ops = banks[4 + ((h // 2) % 2)]
                    osl = slice((h % 2) * 256, (h % 2 + 1) * 256)
                    fw.mm(ops[:, osl], sT, v4[:, t, h, 0:256], True, False, excl=(h % 2 == 0))
                    fw.mm(ops[:, osl], qTh, Cbc[:, h, 0:256], False, True)
                    fw.mm(dps[:, h:h + 1], sT, onesb[:, 0:1], True, False, excl=(h == 0))
                    fw.mm(dps[:, h:h + 1], qTh, Cbc[:, h, 256:257], False, True)
                    ups = banks[6]
                    fw.mm(ups[:, 0:257], kw, v4[:, t, h, 0:257])
                    fw.stt(Cm3[:, h, 0:257], Cm3[:, h, 0:257], EX[:, 24 + h:25 + h], ups[:, 0:257], ALU.mult, ALU.add, add=True)
                    fw.copy(POOL, Cbn[:, h, 0:257], Cm3[:, h, 0:257], add=True)
                    fw.copy(ACT, hb3[:, h, :], ops[:, osl], add=True)
                    fw.act(junks[h], ops[:, osl], AF.Square, accum=ss[:, h:h + 1], accum_add=True)
                dec = EX[:, 16:24]
                fw.tt(DVE, dt[:, 0:8], dps[:, 0:8], dec, ALU.mult)
                fw.ts(DVE, dt[:, 8:16], dt[:, 0:8], -1.0, None, op0=ALU.mult)
                fw.tt(DVE, dt[:, 8:16], dt[:, 0:8], dt[:, 8:16], ALU.max)
                fw.ts(DVE, dt[:, 8:16], dt[:, 8:16], 1.0, None, op0=ALU.max)
                fw.recip(dt[:, 16:24], dt[:, 8:16])
                fw.tt(DVE, dt[:, 24:32], dt[:, 16:24], dec, ALU.mult)
                fw.tt(DVE, dt[:, 32:40], dt[:, 24:32], dt[:, 24:32], ALU.mult)
                fw.tt(DVE, dt[:, 40:48], ss, dt[:, 32:40], ALU.mult)
                fw.ts(DVE, dt[:, 40:48], dt[:, 40:48], 1.0 / 256, EPS, op0=ALU.mult, op1=ALU.add)
                fw.act(dt[:, 48:56], dt[:, 40:48], AF.Ln)
                fw.act(dt[:, 48:56], dt[:, 48:56], AF.Exp, scale=-0.5)
                fw.tt(DVE, dt[:, 56:64], dt[:, 24:32], dt[:, 48:56], ALU.mult)
                for h in range(8):
                    fw.stt(u[:, h * 256:(h + 1) * 256], hb3[:, h, :], dt[:, 56 + h:57 + h], oz3[:, t, h * 256:(h + 1) * 256],
                           ALU.mult, ALU.mult, add=(h > 0))
                for g in range(4):
                    pb = banks[2 + g % 2].v(lambda a: a.bitcast(BF16))
                    for j in range(4):
                        c = g * 4 + j
                        fw.transpose_add(pb[:, j * 128:(j + 1) * 128], u[:, c * 128:(c + 1) * 128], ident, sig=(j == 3))
                    for j in range(4):
                        c = g * 4 + j
                        if g % 2 == 0:
                            fw.act(uT3[:, c, t * 128:(t + 1) * 128], pb[:, j * 128:(j + 1) * 128], AF.Copy,
                                   scale=gmT[:, c:c + 1], add=True)
                        else:
                            fw.ts(DVE, uT3[:, c, t * 128:(t + 1) * 128], pb[:, j * 128:(j + 1) * 128], gmT[:, c:c + 1], None,
                                  op0=ALU.mult, add=True)
            for c in range(16):
                fw.dma(STQ, uaT_sc[c * 128:(c + 1) * 128, tok0:tok0 + 512], uT3[:, c, :], add=True)
        fw.barrier()
        fw.release(par_mark)

        if stop == 'C':
            break
        wuq = fw.alloc("wuq", 4 * 3072, BF16)
        wuq3 = r3(wuq, "p (c n) -> p c n", c=4)
        wuq4 = r3(wuq, "p (c h n) -> p c h n", c=4, h=16)
        wrot = fw.alloc("wrot", 4 * 1024, BF16)
        wrot4 = r3(wrot, "p (c h n) -> p c h n", c=4, h=16)
        wukv = fw.alloc("wukv", 4 * 4096, BF16)
        wukv3 = r3(wukv, "p (c n) -> p c n", c=4)
        wukv4 = r3(wukv, "p (c h n) -> p c h n", c=4, h=16)
        stg = [fw.alloc(f"stg{i}", 4 * 512, F32) for i in range(2)]
        k = 0
        for (wsrc, wdst, ncol) in ((Wl["w_uq"], wuq3, 3072), (Wl["w_ukv"], wukv3, 4096)):
            for j in range(ncol // 512):
                s_ = r3(stg[k % 2], "p (c n) -> p c n", c=4)
                k += 1
                fw.dma(SP, s_, wsrc[:, j * 512:(j + 1) * 512].v(lambda a: a.rearrange("(c p) n -> p c n", p=128)))
                fw.copy(POOL, wdst[:, :, j * 512:(j + 1) * 512], s_, add=(j > 0))
        for c in range(4):
            fw.ts(POOL, wrot4[:, c, :, 0:32], wuq4[:, c, :, 160:192], -1.0, None, op0=ALU.mult, add=(c > 0))
            fw.copy(POOL, wrot4[:, c, :, 32:64], wuq4[:, c, :, 128:160], add=True)
        cq3s = [r3(fw.alloc(f"cq{i}", 4 * 512, BF16), "p (c t) -> p c t", c=4) for i in range(2)]
        ckv3s = [r3(fw.alloc(f"ckv{i}", 4 * 512, BF16), "p (c t) -> p c t", c=4) for i in range(2)]
        krs = [fw.alloc(f"kr{i}", 512, F32, parts=64) for i in range(2)]
        krots = [fw.alloc(f"krot{i}", 512, F32, parts=64) for i in range(2)]
        coss = [fw.alloc(f"cos{i}", 512, F32, parts=64) for i in range(2)]
        sins = [fw.alloc(f"sin{i}", 512, F32, parts=64) for i in range(2)]
        kro = fw.alloc("kro", 512, F32, parts=64)
        rt1 = fw.alloc("rt1", 512, F32, parts=64)
        rt2 = fw.alloc("rt2", 512, F32, parts=64)
        sqkr = fw.alloc("sqkr", 512, BF16, parts=64)
        sqn = [fw.alloc(f"sqn{i}", 512, BF16) for i in range(2)]
        sqr = [fw.alloc(f"sqr{i}", 512, BF16, parts=64) for i in range(2)]
        lnt = [fw.alloc(f"lnt{i}", 512, F32) for i in range(2)]
        rsd = [fw.alloc(f"rsd{i}", 512, F32) for i in range(2)]
        onT = [fw.alloc(f"onT{i}", 512, BF16) for i in range(4)]
        orT = [fw.alloc(f"orT{i}", 512, BF16, parts=64) for i in range(4)]
        vo = [fw.alloc(f"vo{i}", 2048, BF16) for i in range(2)]
        gq, gk = gn["q"], gn["k"]
        kk = 0
        for b5 in range(NB5):
            tsl = slice(b5 * 512, (b5 + 1) * 512)
            cq3, ckv3 = cq3s[b5 % 2], ckv3s[b5 % 2]
            kr, krot, cs_, sn_ = krs[b5 % 2], krots[b5 % 2], coss[b5 % 2], sins[b5 % 2]
            fw.dma(SP, cq3, cqT_sc[:, tsl].v(lambda a: a.rearrange("(c p) t -> p c t", p=128)))
            fw.dma(SP, ckv3, ckvT_sc[:, tsl].v(lambda a: a.rearrange("(c p) t -> p c t", p=128)))
            fw.dma(SP, kr, kr_sc[:, tsl])
            fw.dma(SP, krot, krot_sc[:, tsl])
            fw.dma(SP, cs_, cos_sc[:, tsl])
            fw.dma(SP, sn_, sin_sc[:, tsl])
            fw.stt(rt1, kr, gk[0:64, 1:2], cs_, ALU.mult, ALU.mult)
            fw.stt(rt2, krot, gk[0:64, 2:3], sn_, ALU.mult, ALU.mult)
            fw.tt(DVE, kro, rt1, rt2, ALU.add)
            fw.act(sqkr, kr, AF.Square)
            for h in range(16):
                qn_ps, qr_ps, qx_ps = nbank(), nbank(), nbank()
                for c in range(4):
                    fw.mm(qn_ps, wuq3[:, c, h * 192:h * 192 + 128], cq3[:, c, :], start=(c == 0), stop=(c == 3))
                for c in range(4):
                    fw.mm(qr_ps[0:64, :], wuq3[:, c, h * 192 + 128:h * 192 + 192], cq3[:, c, :], start=(c == 0), stop=(c == 3))
                for c in range(4):
                    fw.mm(qx_ps[0:64, :], wrot4[:, c, h, :], cq3[:, c, :], start=(c == 0), stop=(c == 3))
                a_n, a_r, l_t, r_d = sqn[kk % 2], sqr[kk % 2], lnt[kk % 2], rsd[kk % 2]
                o_n, o_r = onT[kk % 4], orT[kk % 4]
                kk += 1
                fw.act(a_n, qn_ps, AF.Square)
                fw.act(a_r, qr_ps[0:64, :], AF.Square)
                ss_ps = nbank()
                fw.mm(ss_ps, onesb, a_n, True, False)
                fw.mm(ss_ps, onesb[0:64, :], a_r, False, True)
                fw.ts(DVE, l_t, ss_ps, 1.0 / 192, EPS, op0=ALU.mult, op1=ALU.add)
                fw.act(l_t, l_t, AF.Ln)
                fw.act(r_d, l_t, AF.Exp, scale=-0.5)
                fw.stt(o_n, qn_ps, gq[:, 0:1], r_d, ALU.mult, ALU.mult)
                fw.stt(rt1, qr_ps[0:64, :], gq[0:64, 1:2], cs_, ALU.mult, ALU.mult, deps=(a_r.buf,))
                fw.stt(rt2, qx_ps[0:64, :], gq[0:64, 2:3], sn_, ALU.mult, ALU.mult)
                fw.tt(DVE, rt1, rt1, rt2, ALU.add)
                fw.tt(DVE, o_r, rt1, r_d[0:64, :], ALU.mult)
                fw.dma(STQ, QT_sc[h, 0:128, tsl], o_n, add=True)
                fw.dma(STQ, QT_sc[h, 128:192, tsl], o_r, add=True)
                kn_ps = nbank()
                for c in range(4):
                    fw.mm(kn_ps, wukv3[:, c, h * 256:h * 256 + 128], ckv3[:, c, :], start=(c == 0), stop=(c == 3))
                a_n, l_t, r_d = sqn[kk % 2], lnt[kk % 2], rsd[kk % 2]
                o_n, o_r = onT[kk % 4], orT[kk % 4]
                kk += 1
                fw.act(a_n, kn_ps, AF.Square)
                ss_ps = nbank()
                fw.mm(ss_ps, onesb, a_n, True, False)
                fw.mm(ss_ps, onesb[0:64, :], sqkr, False, True)
                fw.ts(DVE, l_t, ss_ps, 1.0 / 192, EPS, op0=ALU.mult, op1=ALU.add)
                fw.act(l_t, l_t, AF.Ln)
                fw.act(r_d, l_t, AF.Exp, scale=-0.5)
                fw.stt(o_n, kn_ps, gk[:, 0:1], r_d, ALU.mult, ALU.mult)
                fw.tt(DVE, o_r, kro, r_d[0:64, :], ALU.mult)
                fw.dma(STQ, KT_sc[h, 0:128, tsl], o_n, add=True)
                fw.dma(STQ, KT_sc[h, 128:192, tsl], o_r, add=True)
            for t in range(4):
                vot = vo[t % 2]
                for g in range(4):
                    ps = nbank()
                    for c in range(4):
                        fw.mm(ps, ckv3[:, c, t * 128:(t + 1) * 128], wukv4[:, c, g * 4:(g + 1) * 4, 128:256],
                              start=(c == 0), stop=(c == 3))
                    fw.copy(ACT if g % 2 == 0 else DVE, vot[:, g * 512:(g + 1) * 512], ps, add=(g > 0))
                fw.dma(STQ, V_sc[b5 * 512 + t * 128:b5 * 512 + (t + 1) * 128, :], vot, add=True)
        fw.barrier()
        fw.release(par_mark)

        if stop == 'D':
            break
        Kn = [fw.alloc(f"Kn{i}", S, BF16) for i in range(2)]
        Kr = [fw.alloc(f"Kr{i}", S, BF16, parts=64) for i in range(2)]
        Vh3 = [r3(fw.alloc(f"Vh{i}", NT * 128, BF16), "p (t e) -> p t e", t=NT) for i in range(2)]
        Qn = [fw.alloc(f"Qn{i}", 512, BF16) for i in range(2)]
        Qr = [fw.alloc(f"Qr{i}", 512, BF16, parts=64) for i in range(2)]
        zb = [fw.alloc(f"zb{i}", 512, BF16) for i in range(2)]
        Pt = [fw.alloc(f"Pt{i}", 512, BF16) for i in range(4)]
        Dsb = [fw.alloc(f"Dsb{i}", 512, F32) for i in range(2)]
        rz = [fw.alloc(f"rz{i}", 512, F32) for i in range(2)]
        uo = [fw.alloc(f"uo{i}", 512, BF16) for i in range(2)]
        kc = 0
        for h in range(16):
            Knh, Krh, V3 = Kn[h % 2], Kr[h % 2], Vh3[h % 2]
            fw.dma(SP, Knh, KT_sc[h, 0:128, :])
            fw.dma(SP, Krh, KT_sc[h, 128:192, :])
            fw.dma(SP, V3, V_sc[:, h * 128:(h + 1) * 128].v(lambda a: a.rearrange("(t p) e -> p t e", p=128)))
            for qb in range(NB5):
                i = h * NB5 + qb
                qsl = slice(qb * 512, (qb + 1) * 512)
                Qni, Qri, zbi = Qn[i % 2], Qr[i % 2], zb[i % 2]
                fw.dma(SP, Qni, QT_sc[h, 0:128, qsl])
                fw.dma(SP, Qri, QT_sc[h, 128:192, qsl])
                fw.dma(SP, zbi, szbT_sc[h * 128:(h + 1) * 128, qsl])
                Ops, Dps = banks[4 + i % 2], banks[6 + i % 2]
                nk = 4 * qb + 4

                def smat(kt):
                    q0 = max(0, kt - 4 * qb) * 128
                    sp_ = banks[(kc + kt) % 4]
                    fw.mm(sp_[:, q0:512], Knh[:, kt * 128:(kt + 1) * 128], Qni[:, q0:512], True, False)
                    fw.mm(sp_[:, q0:512], Krh[0:64, kt * 128:(kt + 1) * 128], Qri[0:64, q0:512], False, True)

                smat(0)
                for kt in range(nk):
                    if kt + 1 < nk:
                        smat(kt + 1)
                    r_ = kt - 4 * qb
                    q0 = max(0, r_) * 128
                    sp_ = banks[(kc + kt) % 4]
                    P = Pt[(kc + kt) % 4]
                    fw.act(P[:, q0:512], sp_[:, q0:512], AF.Exp, scale=ATT_SCALE)
                    if r_ >= 0:
                        fw.tt(POOL, P[:, q0:q0 + 128], P[:, q0:q0 + 128], trib, ALU.mult)
                    fw.mm(Ops[:, q0:512], V3[:, kt, :], P[:, q0:512], start=(kt == 0), stop=(kt == nk - 1))
                    fw.mm(Dps[:, q0:512], onesb, P[:, q0:512], start=(kt == 0), stop=(kt == nk - 1))
                kc += nk
                d_, z_, o_ = Dsb[i % 2], rz[i % 2], uo[i % 2]
                fw.copy(ACT, d_, Dps)
                fw.recip(d_, d_)
                fw.tt(DVE, z_, zbi, d_, ALU.mult)
                fw.tt(DVE, o_, Ops, z_, ALU.mult)
                fw.dma(STQ, ubT_sc[h * 128:(h + 1) * 128, qsl], o_, add=True)
        fw.barrier()
        fw.release(par_mark)

        if stop == 'E':
            break
        jobs = []
        for b5 in range(NB5):
            for g in range(8):
                jobs.append((Wl["w_a"], g * 256, 256))
                jobs.append((Wl["w_b"], g * 256, 256))
            for g in range(8):
                jobs.append((Wl["w_out"], g * 256, 256))
        wsx = WStream(fw, jobs)
        ua3s = [r3(fw.alloc(f"uaT{i}", 16 * 512, BF16), "p (c t) -> p c t", c=16) for i in range(1)]
        ub3s = [r3(fw.alloc(f"ubT{i}", 16 * 512, BF16), "p (c t) -> p c t", c=16) for i in range(1)]
        yT3 = r3(fw.alloc("yT", 16 * 512, BF16), "p (c t) -> p c t", c=16)
        xres = [fw.alloc(f"xres{i}", D, F32) for i in range(4)]
        sga = [fw.alloc(f"sga{i}", 512, BF16) for i in range(2)]
        sgb = [fw.alloc(f"sgb{i}", 512, BF16) for i in range(2)]
        ft1 = [fw.alloc(f"ft1{i}", 512, F32) for i in range(2)]
        ft2 = [fw.alloc(f"ft2{i}", 512, F32) for i in range(2)]
        kk = 0
        for b5 in range(NB5):
            tsl = slice(b5 * 512, (b5 + 1) * 512)
            ua3, ub3 = ua3s[0], ub3s[0]
            fw.dma(SP, ua3, uaT_sc[:, tsl].v(lambda a: a.rearrange("(c p) t -> p c t", p=128)))
            fw.dma(SP, ub3, ubT_sc[:, tsl].v(lambda a: a.rearrange("(c p) t -> p c t", p=128)))
            for g in range(8):
                wa = wsx.get(ahead=2)
                wb_ = wsx.get(ahead=1)
                for cc2 in range(2):
                    d = g * 2 + cc2
                    sa, sb_, f1, f2 = sga[kk % 2], sgb[kk % 2], ft1[kk % 2], ft2[kk % 2]
                    kk += 1
                    fw.dma(SP, sa, sgaT_sc[d * 128:(d + 1) * 128, tsl])
                    fw.dma(SP, sb_, sgbT_sc[d * 128:(d + 1) * 128, tsl])
                    pa, pb_ = nbank(), nbank()
                    for c in range(16):
                        fw.mm(pa, wa[:, c, cc2 * 128:(cc2 + 1) * 128], ua3[:, c, :], start=(c == 0), stop=(c == 15))
                    for c in range(16):
                        fw.mm(pb_, wb_[:, c, cc2 * 128:(cc2 + 1) * 128], ub3[:, c, :], start=(c == 0), stop=(c == 15))
                    fw.tt(DVE, f1, sa, pa, ALU.mult)
                    fw.tt(DVE, f2, sb_, pb_, ALU.mult)
                    fw.tt(POOL, yT3[:, d, :], f1, f2, ALU.add, add=(d > 0))
            for t in range(4):
                fw.dma(SP, xres[t], x_src[b5 * 512 + t * 128:b5 * 512 + (t + 1) * 128, :])
            for eg in range(8):
                wo = wsx.get()
                for t in range(4):
                    ps = nbank()
                    for d in range(16):
                        fw.mm(ps[:, 0:256], yT3[:, d, t * 128:(t + 1) * 128], wo[:, d, :], start=(d == 0), stop=(d == 15))
                    xs_ = xres[t][:, eg * 256:(eg + 1) * 256]
                    fw.tt(DVE, xs_, ps[:, 0:256], xs_, ALU.add, add=True)
            for t in range(4):
                fw.dma(STQ, x_dst[b5 * 512 + t * 128:b5 * 512 + (t + 1) * 128, :], xres[t], add=True)
        fw.barrier()
        fw.release(lay_mark)

    fw.finish()
    return nc, st, fw


_CACHE = {}


def _inv_freq():
    f = np.exp(-math.log(10000.0) * np.arange(0, 64, 2, dtype=np.float32) / np.float32(64)).astype(np.float32)
    return np.concatenate([f, f]).reshape(64, 1).astype(np.float32)


def run_layers(x, positions, weights, l0, l1):
    B, S, _ = x.shape
    NL = l1 - l0
    key = (S, NL)
    if key not in _CACHE:
        _CACHE[key] = build_program(S, NL)
    nc, st, fw = _CACHE[key]
    wsl = {n: np.ascontiguousarray(weights[n][l0:l1]) for n in W_NAMES}
    invf = _inv_freq()
    in_maps = []
    for b in range(B):
        m = {"x": np.ascontiguousarray(x[b]), "pos": np.ascontiguousarray(positions[b:b + 1]).astype(np.int32), "invf": invf}
        m.update(wsl)
        in_maps.append(m)
    res = run_bass_kernel_spmd(nc, in_maps, core_ids=list(range(B)))
    return np.stack([np.asarray(r["out"]) for r in res.results], axis=0)


LAYERS_PER_LAUNCH = 4


def kernel(x, positions, norm_g, w_in, gate_bias, conv_w, mlstm_norm_g, w_a, q_lat_g, kv_lat_g,
           w_uq, w_ukv, q_norm_g, k_norm_g, w_b, w_out):
    weights = {"norm_g": norm_g, "w_in": w_in, "gate_bias": gate_bias, "conv_w": conv_w, "mlstm_norm_g": mlstm_norm_g,
               "w_a": w_a, "q_lat_g": q_lat_g, "kv_lat_g": kv_lat_g, "w_uq": w_uq, "w_ukv": w_ukv,
               "q_norm_g": q_norm_g, "k_norm_g": k_norm_g, "w_b": w_b, "w_out": w_out}
    weights = {k: np.asarray(v, dtype=np.float32) for k, v in weights.items()}
    x = np.asarray(x, dtype=np.float32)
    positions = np.asarray(positions, dtype=np.int32)
    depth = w_in.shape[0]
    for l0 in range(0, depth, LAYERS_PER_LAUNCH):
        x = run_layers(x, positions, weights, l0, min(depth, l0 + LAYERS_PER_LAUNCH))
    return x.astype(np.float32)
```

```python
import numpy as np
import math
from contextlib import ExitStack
import concourse.bass as bass
import concourse.mybir as mybir
from concourse.bass_utils import run_bass_kernel_spmd

F32 = mybir.dt.float32
BF16 = mybir.dt.bfloat16
I32 = mybir.dt.int32
AF = mybir.ActivationFunctionType
ALU = mybir.AluOpType
AX = mybir.AxisListType


class Buf:
    __slots__ = ("w", "r", "w0", "name")

    def __init__(self, name=""):
        self.w = {}
        self.r = {}
        self.w0 = {}
        self.name = name


class T:
    __slots__ = ("ap", "buf")

    def __init__(self, ap, buf):
        self.ap = ap
        self.buf = buf

    def __getitem__(self, k):
        return T(self.ap[k], self.buf)

    def v(self, fn):
        return T(fn(self.ap), self.buf)


class Eng:
    def __init__(self, name, sem_id, dma_sem_ids):
        self.name = name
        self.sem = sem_id
        self.cnt = 0
        self.prog = []
        self.waited = {}
        self.dma_sems = dma_sem_ids
        self.dma_i = 0
        self.dma_val = {s: 0 for s in dma_sem_ids}
        self.pend = []
        self.ninstr = 0


class FW:
    NDMA = 8

    def __init__(self, nc, stack, arena_bytes):
        self.nc = nc
        self.sems = []
        def newsem(name):
            h = stack.enter_context(nc.semaphore(name))
            self.sems.append(h)
            return len(self.sems) - 1
        self.PE = Eng("PE", newsem("s_pe"), [])
        self.ACT = Eng("ACT", newsem("s_act"), [newsem(f"s_actq{i}") for i in range(self.NDMA)])
        self.DVE = Eng("DVE", newsem("s_dve"), [])
        self.POOL = Eng("POOL", newsem("s_pool"), [newsem(f"s_poolq{i}") for i in range(self.NDMA)])
        self.SP = Eng("SP", newsem("s_sp"), [newsem(f"s_spq{i}") for i in range(self.NDMA)])
        self.engs = [self.PE, self.ACT, self.DVE, self.POOL, self.SP]
        self.arena = stack.enter_context(nc.sbuf_tensor("arena", [128, arena_bytes // 4], F32))
        self.arena_bytes = arena_bytes
        self.top = 0
        self.banks = []
        for i in range(8):
            p = stack.enter_context(nc.psum_tensor(f"psb{i}", [128, 512], F32))
            self.banks.append(T(p[:, :], Buf(f"psb{i}")))

    def alloc(self, name, cols, dtype, parts=128):
        esz = 2 if dtype == BF16 else 4
        nbytes = (cols * esz + 31) // 32 * 32
        assert self.top + nbytes <= self.arena_bytes, f"arena overflow at {name}: {self.top}+{nbytes}"
        a = self.arena[0:parts, self.top // 4:(self.top + nbytes) // 4]
        if dtype != F32:
            a = a.bitcast(dtype)
        a = a[:, 0:cols]
        self.top += nbytes
        return T(a, Buf(name))

    def mark(self):
        return self.top

    def release(self, m):
        self.top = m

    def dram(self, name, shape, dtype, kind="Internal"):
        t = self.nc.dram_tensor(name, list(shape), dtype, kind=kind)
        return T(t.ap(), Buf(name))

    def op(self, E, fn, reads=(), writes=(), adds=(), sig=True, dma=False):
        waits = {}

        def need(d, skip_own=False):
            for sem, val in d.items():
                if skip_own and sem == E.sem:
                    continue
                if E.waited.get(sem, 0) < val:
                    if waits.get(sem, 0) < val:
                        waits[sem] = val

        pe = E is self.PE
        if not pe and self.PE.pend:
            for (prs, pws, pas) in self.PE.pend:
                for b in list(writes) + list(adds):
                    assert all(b is not x for x in prs), f"pending PE read hazard on {b.name}"
        for b in reads:
            need(b.w, skip_own=pe)
        for b in writes:
            need(b.w, skip_own=pe)
            need(b.r, skip_own=True)
        for b in adds:
            need(b.r, skip_own=True)
            need(b.w0, skip_own=pe)
        tok = None
        if dma:
            sem = E.dma_sems[E.dma_i]
            E.dma_i = (E.dma_i + 1) % len(E.dma_sems)
            prev = E.dma_val[sem]
            if prev > 0:
                need({sem: prev})
            E.dma_val[sem] = prev + 16
            tok = (sem, prev + 16)
        elif sig:
            E.cnt += 1
            tok = (E.sem, E.cnt)
        for sem, val in waits.items():
            E.waited[sem] = val
        wl = [(self.sems[s], v) for s, v in waits.items()]
        if dma:
            semh = self.sems[tok[0]]

            def run(e, wl=wl, fn=fn, semh=semh):
                for h, v in wl:
                    e.wait_ge(h, v)
                fn(e).then_inc(semh, 16)
        elif sig:
            semh = self.sems[E.sem]

            def run(e, wl=wl, fn=fn, semh=semh):
                for h, v in wl:
                    e.wait_ge(h, v)
                fn(e).then_inc(semh, 1)
        else:
            def run(e, wl=wl, fn=fn):
                for h, v in wl:
                    e.wait_ge(h, v)
                fn(e)
        E.prog.append(run)
        E.ninstr += 1
        if tok is None:
            E.pend.append((reads, writes, adds))
            return
        groups = [(reads, writes, adds)]
        if not dma and E.pend:
            groups += E.pend
            E.pend = []
        s, v = tok
        for (rs, ws, ads) in groups:
            for b in rs:
                if b.r.get(s, 0) < v:
                    b.r[s] = v
            for b in ws:
                b.w = {s: v}
                b.w0 = {s: v}
                b.r = {}
            for b in ads:
                if b.w.get(s, 0) < v:
                    b.w[s] = v

    def barrier(self):
        allt = {}
        for E in self.engs:
            if E.cnt > 0:
                allt[E.sem] = E.cnt
            for s, v in E.dma_val.items():
                if v > 0:
                    allt[s] = v
        for E in self.engs:
            assert not E.pend
            waits = {}
            for s, v in allt.items():
                if s == E.sem:
                    continue
                if E.waited.get(s, 0) < v:
                    waits[s] = v
                    E.waited[s] = v
            wl = [(self.sems[s], v) for s, v in waits.items()]
            if wl:
                def run(e, wl=wl):
                    for h, v in wl:
                        e.wait_ge(h, v)
                E.prog.append(run)

    def finish(self):
        self.barrier()
        nc = self.nc
        with nc.Block() as block:
            @block.tensor
            def _(e):
                for f in self.PE.prog:
                    f(e)

            @block.scalar
            def _(e):
                for f in self.ACT.prog:
                    f(e)

            @block.vector
            def _(e):
                for f in self.DVE.prog:
                    f(e)

            @block.gpsimd
            def _(e):
                for f in self.POOL.prog:
                    f(e)

            @block.sync
            def _(e):
                for f in self.SP.prog:
                    f(e)

    def mm(self, out, lhsT, rhs, start=True, stop=True, excl=None):
        o, l, r = out.ap, lhsT.ap, rhs.ap
        fn = lambda e: e.matmul(o, l, r, start=start, stop=stop)
        if excl is None:
            excl = start
        if excl:
            self.op(self.PE, fn, reads=(lhsT.buf, rhs.buf), writes=(out.buf,), sig=stop)
        else:
            self.op(self.PE, fn, reads=(lhsT.buf, rhs.buf), adds=(out.buf,), sig=stop)

    def transpose(self, out, in_, ident):
        o, i, d = out.ap, in_.ap, ident.ap
        self.op(self.PE, lambda e: e.transpose(o, i, d), reads=(in_.buf, ident.buf), writes=(out.buf,))

    def transpose_add(self, out, in_, ident, sig=True):
        o, i, d = out.ap, in_.ap, ident.ap
        self.op(self.PE, lambda e: e.transpose(o, i, d), reads=(in_.buf, ident.buf), adds=(out.buf,), sig=sig)

    def act(self, out, in_, func, bias=None, scale=None, accum=None, add=False, accum_add=False):
        o, i = out.ap, in_.ap
        kw = {}
        reads = [in_.buf]
        writes = [out.buf]
        if bias is not None:
            if isinstance(bias, T):
                kw["bias"] = bias.ap
                reads.append(bias.buf)
            else:
                kw["bias"] = bias
        if scale is not None:
            if isinstance(scale, T):
                kw["scale"] = scale.ap
                reads.append(scale.buf)
            else:
                kw["scale"] = scale
        adds = []
        if accum is not None:
            kw["accum_out"] = accum.ap
            (adds if accum_add else writes).append(accum.buf)
        if add:
            self.op(self.ACT, lambda e: e.activation(o, i, func, **kw), reads=reads, adds=writes + adds)
        else:
            self.op(self.ACT, lambda e: e.activation(o, i, func, **kw), reads=reads, writes=writes, adds=adds)

    def _e(self, E):
        return E

    def tt(self, E, out, in0, in1, op, add=False):
        o, a, b = out.ap, in0.ap, in1.ap
        fn = lambda e: e.tensor_tensor(out=o, in0=a, in1=b, op=op)
        if add:
            self.op(E, fn, reads=(in0.buf, in1.buf), adds=(out.buf,))
        else:
            self.op(E, fn, reads=(in0.buf, in1.buf), writes=(out.buf,))

    def ts(self, E, out, in0, s1, s2=None, op0=ALU.mult, op1=None, add=False):
        o, a = out.ap, in0.ap
        reads = [in0.buf]
        if isinstance(s1, T):
            reads.append(s1.buf)
            s1 = s1.ap
        if isinstance(s2, T):
            reads.append(s2.buf)
            s2 = s2.ap
        if op1 is None:
            fn = lambda e: e.tensor_scalar(out=o, in0=a, scalar1=s1, scalar2=None, op0=op0)
        else:
            fn = lambda e: e.tensor_scalar(out=o, in0=a, scalar1=s1, scalar2=s2, op0=op0, op1=op1)
        if add:
            self.op(E, fn, reads=reads, adds=(out.buf,))
        else:
            self.op(E, fn, reads=reads, writes=(out.buf,))

    def stt(self, out, in0, scalar, in1, op0, op1, add=False, deps=()):
        o, a, b = out.ap, in0.ap, in1.ap
        reads = [in0.buf, in1.buf] + list(deps)
        if isinstance(scalar, T):
            reads.append(scalar.buf)
            scalar = scalar.ap
        fn = lambda e: e.scalar_tensor_tensor(out=o, in0=a, scalar=scalar, in1=b, op0=op0, op1=op1)
        if add:
            self.op(self.DVE, fn, reads=reads, adds=(out.buf,))
        else:
            self.op(self.DVE, fn, reads=reads, writes=(out.buf,))

    def copy(self, E, out, in_, add=False):
        o, i = out.ap, in_.ap
        if E is self.ACT:
            fn = lambda e: e.copy(o, i)
        else:
            fn = lambda e: e.tensor_copy(out=o, in_=i)
        if add:
            self.op(E, fn, reads=(in_.buf,), adds=(out.buf,))
        else:
            self.op(E, fn, reads=(in_.buf,), writes=(out.buf,))

    def recip(self, out, in_):
        o, i = out.ap, in_.ap
        self.op(self.DVE, lambda e: e.reciprocal(o, i), reads=(in_.buf,), writes=(out.buf,))

    def memset(self, E, out, val, add=False):
        o = out.ap
        if add:
            self.op(E, lambda e: e.memset(o, val), adds=(out.buf,))
        else:
            self.op(E, lambda e: e.memset(o, val), writes=(out.buf,))

    def dma(self, E, out, in_, add=False, **kw):
        o, i = out.ap, in_.ap
        fn = lambda e: e.dma_start(out=o, in_=i, **kw)
        if add:
            self.op(E, fn, reads=(in_.buf,), adds=(out.buf,), dma=True)
        else:
            self.op(E, fn, reads=(in_.buf,), writes=(out.buf,), dma=True)


D = 2048
NIN = 15440
O_QA, O_KA, O_VA, O_OA, O_IA, O_FA, O_ZA = 0, 1024, 2048, 4096, 6144, 6152, 6160
O_CQ, O_CKV, O_KR, O_ZB, O_GA, O_GB = 8208, 8720, 9232, 9296, 11344, 13392
EPS = 1e-6
LN_DK = math.log(128.0 ** -0.5)
ATT_SCALE = 192.0 ** -0.5
TWO_PI_LO = 6.2831845
W_NAMES = ["norm_g", "w_in", "gate_bias", "conv_w", "mlstm_norm_g", "w_a", "q_lat_g", "kv_lat_g",
           "w_uq", "w_ukv", "q_norm_g", "k_norm_g", "w_b", "w_out"]
W_SHAPES = {"norm_g": [D], "w_in": [D, NIN], "gate_bias": [16], "conv_w": [4, 2048], "mlstm_norm_g": [2048],
            "w_a": [2048, D], "q_lat_g": [512], "kv_lat_g": [512], "w_uq": [512, 3072], "w_ukv": [512, 4096],
            "q_norm_g": [192], "k_norm_g": [192], "w_b": [2048, D], "w_out": [D, D]}


def r3(_x, pat, **kw):
    return _x.v(lambda a: a.rearrange(pat, **kw))


class WStream:
    def __init__(self, fw, jobs, nst=2, nbf=4, width=256, cast=None):
        self.fw = fw
        self.jobs = jobs
        self.cast = cast if cast is not None else [fw.POOL]
        self.st = [fw.alloc(f"wst{i}", 16 * width, F32) for i in range(nst)]
        self.bf = [fw.alloc(f"wbf{i}", 16 * width, BF16) for i in range(nbf)]
        self.issued = 0
        self.taken = 0
        self.tiles = {}

    def _issue(self):
        i = self.issued
        if i >= len(self.jobs):
            return
        src, col0, n = self.jobs[i]
        fw = self.fw
        ws = r3(self.st[i % len(self.st)][:, 0:16 * n], "p (c n) -> p c n", c=16)
        wb = r3(self.bf[i % len(self.bf)][:, 0:16 * n], "p (c n) -> p c n", c=16)
        fw.dma(fw.SP, ws, src[:, col0:col0 + n].v(lambda a: a.rearrange("(c p) n -> p c n", p=128)))
        fw.copy(self.cast[i % len(self.cast)], wb, ws)
        self.tiles[i] = wb
        self.issued += 1

    def get(self, ahead=1):
        while self.issued <= self.taken + ahead and self.issued < len(self.jobs):
            self._issue()
        t = self.tiles.pop(self.taken)
        self.taken += 1
        return t


def build_program(S, NL, debug=False, stop=None):
    nc = bass.Bass("TRN2", target_bir_lowering=False)
    st = ExitStack()
    fw = FW(nc, st, 188 * 1024)
    PE, ACT, DVE, POOL, SP = fw.PE, fw.ACT, fw.DVE, fw.POOL, fw.SP
    STQ = SP
    banks = fw.banks
    NT = S // 128
    TB = min(S, 2048)
    NTB = S // TB
    NTT = TB // 128
    NSB = TB // 512
    NB5 = S // 512
    skind = "ExternalOutput" if debug else "Internal"

    x_in = fw.dram("x", [S, D], F32, kind="ExternalInput")
    pos = fw.dram("pos", [1, S], I32, kind="ExternalInput")
    invf = fw.dram("invf", [64, 1], F32, kind="ExternalInput")
    Wd = {n: fw.dram(n, [NL] + W_SHAPES[n], F32, kind="ExternalInput") for n in W_NAMES}
    out_d = fw.dram("out", [S, D], F32, kind="ExternalOutput")
    xmid = [fw.dram(f"xmid{i}", [S, D], F32) for i in range(max(NL - 1, 0))]

    cos_sc = fw.dram("cos_sc", [64, S], F32, kind=skind)
    sin_sc = fw.dram("sin_sc", [64, S], F32, kind=skind)
    qkT_sc = fw.dram("qkT_sc", [2048, S], BF16, kind=skind)
    v_sc = fw.dram("v_sc", [S, 2048], BF16, kind=skind)
    oz_sc = fw.dram("oz_sc", [S, 2048], BF16, kind=skind)
    gates_sc = fw.dram("gates_sc", [S, 16], F32, kind=skind)
    cqT_sc = fw.dram("cqT_sc", [512, S], BF16, kind=skind)
    ckvT_sc = fw.dram("ckvT_sc", [512, S], BF16, kind=skind)
    kr_sc = fw.dram("kr_sc", [64, S], F32, kind=skind)
    krot_sc = fw.dram("krot_sc", [64, S], F32, kind=skind)
    szbT_sc = fw.dram("szbT_sc", [2048, S], BF16, kind=skind)
    sgaT_sc = fw.dram("sgaT_sc", [2048, S], BF16, kind=skind)
    sgbT_sc = fw.dram("sgbT_sc", [2048, S], BF16, kind=skind)
    uaT_sc = fw.dram("uaT_sc", [2048, S], BF16, kind=skind)
    QT_sc = fw.dram("QT_sc", [16, 192, S], BF16, kind=skind)
    KT_sc = fw.dram("KT_sc", [16, 192, S], BF16, kind=skind)
    V_sc = fw.dram("V_sc", [S, 2048], BF16, kind=skind)
    ubT_sc = fw.dram("ubT_sc", [2048, S], BF16, kind=skind)

    identf = fw.alloc("identf", 128, F32)
    ident = fw.alloc("ident", 128, BF16)
    trif = fw.alloc("trif", 128, F32)
    trib = fw.alloc("trib", 128, BF16)
    onesf = fw.alloc("onesf", 128, F32)
    onesb = fw.alloc("onesb", 128, BF16)
    fw.memset(POOL, identf, 0.0)
    ia = identf.ap
    fw.op(POOL, lambda e: e.affine_select(out=ia, in_=ia, pattern=[[-1, 128]], compare_op=ALU.not_equal,
                                           fill=1.0, base=0, channel_multiplier=1),
          reads=(identf.buf,), writes=(identf.buf,))
    fw.copy(DVE, ident, identf)
    fw.memset(POOL, onesf, 1.0)
    fw.copy(DVE, onesb, onesf)
    ta, oa = trif.ap, onesf.ap
    fw.op(POOL, lambda e: e.affine_select(out=ta, in_=oa, pattern=[[1, 128]], compare_op=ALU.is_ge,
                                           fill=0.0, base=0, channel_multiplier=-1),
          reads=(onesf.buf,), writes=(trif.buf,))
    fw.copy(DVE, trib, trif)

    pctr = [0]

    def nbank():
        b = banks[pctr[0] % 8]
        pctr[0] += 1
        return b

    m0 = fw.mark()
    CB = min(S, 1024)
    invt = fw.alloc("invt", 1, F32, parts=64)
    fw.dma(SP, invt, invf)
    posi = fw.alloc("posi", CB, I32, parts=64)
    ang = fw.alloc("ang", CB, F32, parts=64)
    tq = fw.alloc("tq", CB, F32, parts=64)
    ki = fw.alloc("ki", CB, I32, parts=64)
    kf = fw.alloc("kf", CB, F32, parts=64)
    rr_ = fw.alloc("rr_", CB, F32, parts=64)
    mk = fw.alloc("mk", CB, F32, parts=64)
    res = [fw.alloc(f"res{i}", CB, F32, parts=64) for i in range(2)]
    k = 0
    for cb in range(S // CB):
        sl = slice(cb * CB, (cb + 1) * CB)
        fw.dma(SP, posi, pos[:, sl].v(lambda a: a.partition_broadcast(64).rearrange("p o s -> p (o s)")))
        fw.copy(DVE, ang, posi)
        fw.ts(DVE, ang, ang, invt[:, 0:1], None, op0=ALU.mult)
        for (off, dst) in ((0.0, sin_sc), (0.25, cos_sc)):
            fw.ts(DVE, tq, ang, 1.0 / (2.0 * math.pi), off, op0=ALU.mult, op1=ALU.add)
            fw.copy(DVE, ki, tq)
            fw.copy(DVE, kf, ki)
            fw.tt(DVE, rr_, tq, kf, ALU.subtract)
            fw.ts(DVE, mk, rr_, 0.5, None, op0=ALU.is_gt)
            fw.tt(DVE, rr_, rr_, mk, ALU.subtract)
            fw.ts(DVE, mk, rr_, -0.5, None, op0=ALU.is_lt)
            fw.tt(DVE, rr_, rr_, mk, ALU.add)
            rs_ = res[k % 2]
            k += 1
            fw.act(rs_, rr_, AF.Sin, scale=TWO_PI_LO)
            fw.dma(SP, dst[:, sl], rs_, add=True)
    fw.barrier()
    fw.release(m0)

    for l in range(NL):
        if stop == 'R':
            break
        x_src = x_in if l == 0 else xmid[l - 1]
        x_dst = out_d if l == NL - 1 else xmid[l]
        Wl = {n: Wd[n][l] for n in W_NAMES}
        w_in = Wl["w_in"]
        lay_mark = fw.mark()

        g_bc = fw.alloc("g_bc", D, F32)
        fw.dma(SP, g_bc, Wl["norm_g"].v(lambda a: a.partition_broadcast(128)))
        convw = fw.alloc("convw", 64, F32)
        convw3 = r3(convw, "p (c j) -> p c j", c=16)
        for c in range(16):
            fw.dma(SP, convw3[:, c, :], Wl["conv_w"][:, c * 128:(c + 1) * 128].v(lambda a: a.rearrange("j p -> p j")),
                   add=(c > 0), allow_slow_non_contiguous=True)
        gmT = fw.alloc("gmT", 16, F32)
        for c4 in range(4):
            fw.dma(SP, gmT[:, c4 * 4:(c4 + 1) * 4],
                   Wl["mlstm_norm_g"][c4 * 512:(c4 + 1) * 512].v(lambda a: a.rearrange("(c p) -> p c", p=128)),
                   add=(c4 > 0), allow_slow_non_contiguous=True)
        glq = fw.alloc("glq", 4, F32)
        fw.dma(SP, glq, Wl["q_lat_g"].v(lambda a: a.rearrange("(c p) -> p c", p=128)), allow_slow_non_contiguous=True)
        glk = fw.alloc("glk", 4, F32)
        fw.dma(SP, glk, Wl["kv_lat_g"].v(lambda a: a.rearrange("(c p) -> p c", p=128)), allow_slow_non_contiguous=True)
        gn = {}
        colv = lambda a: a.rearrange("(p o) -> p o", o=1)
        for nm, src in (("q", Wl["q_norm_g"]), ("k", Wl["k_norm_g"])):
            t = fw.alloc("gn" + nm, 3, F32)
            fw.dma(SP, t[:, 0:1], src[0:128].v(colv), allow_slow_non_contiguous=True)
            fw.dma(SP, t[0:64, 1:2], src[128:192].v(colv), add=True, allow_slow_non_contiguous=True)
            fw.dma(SP, t[0:32, 2:3], src[160:192].v(colv), add=True, allow_slow_non_contiguous=True)
            fw.dma(SP, t[32:64, 2:3], src[128:160].v(colv), add=True, allow_slow_non_contiguous=True)
            gn[nm] = t
        gb_bc = fw.alloc("gb_bc", 16, F32)
        fw.dma(SP, gb_bc, Wl["gate_bias"].v(lambda a: a.partition_broadcast(128)))
        halo = fw.alloc("halo", 48, F32)
        halo3 = r3(halo, "p (c j) -> p c j", c=16)
        fw.memset(POOL, halo, 0.0)
        par_mark = fw.mark()
        if stop == 'P':
            break

        for tb in range(NTB):
            t0 = tb * TB
            fw.release(par_mark)
            hT = fw.alloc("hT", 16 * TB, BF16)
            hT3 = r3(hT, "p (c t) -> p c t", c=16)
            mA = fw.mark()
            xs = [fw.alloc(f"xs{i}", D, F32) for i in range(2)]
            xb = [fw.alloc(f"xb{i}", D, BF16) for i in range(2)]
            stat = [fw.alloc(f"stat{i}", 4, F32) for i in range(2)]
            for tt in range(NTT):
                xt, xbt, sv = xs[tt % 2], xb[tt % 2], stat[tt % 2]
                fw.dma(SP, xt, x_src[t0 + tt * 128:t0 + (tt + 1) * 128, :])
                fw.act(xbt, xt, AF.Square, accum=sv[:, 0:1])
                fw.ts(DVE, sv[:, 1:2], sv[:, 0:1], 1.0 / D, EPS, op0=ALU.mult, op1=ALU.add)
                fw.act(sv[:, 2:3], sv[:, 1:2], AF.Ln)
                fw.act(sv[:, 3:4], sv[:, 2:3], AF.Exp, scale=-0.5)
                fw.stt(xbt, xt, sv[:, 3:4], g_bc, ALU.mult, ALU.mult)
                for g in range(4):
                    pb = nbank().v(lambda a: a.bitcast(BF16))
                    for j in range(4):
                        c = g * 4 + j
                        fw.transpose_add(pb[:, j * 128:(j + 1) * 128], xbt[:, c * 128:(c + 1) * 128], ident, sig=(j == 3))
                    fw.copy(ACT if g % 2 == 0 else DVE, hT3[:, g * 4:(g + 1) * 4, tt * 128:(tt + 1) * 128],
                            r3(pb[:, 0:512], "p (c t) -> p c t", c=4), add=True)
            fw.barrier()
            fw.release(mA)
            if stop == 'A':
                break

            jobs = []
            for g in range(8):
                jobs.append((w_in, O_QA + g * 256, 256))
            for g in range(8):
                jobs.append((w_in, O_VA + g * 256, 256))
            for g in range(8):
                jobs.append((w_in, O_OA + g * 256, 256))
                jobs.append((w_in, O_ZA + g * 256, 256))
            jobs.append((w_in, O_IA, 16))
            jobs += [(w_in, O_CQ, 256), (w_in, O_CQ + 256, 256), (w_in, O_CKV, 256), (w_in, O_CKV + 256, 256)]
            jobs.append((w_in, O_KR, 64))
            for o in (O_ZB, O_GA, O_GB):
                for g in range(8):
                    jobs.append((w_in, o + g * 256, 256))
            wsx = WStream(fw, jobs)
            mB = fw.mark()

            raw = [fw.alloc(f"raw{i}", 3 + TB, F32) for i in range(2)]
            acc = [fw.alloc(f"acc{i}", TB, F32) for i in range(2)]
            ob = [fw.alloc(f"ob{i}", TB, BF16) for i in range(2)]
            for g in range(8):
                wb = wsx.get()
                for cc2 in range(2):
                    cc = g * 2 + cc2
                    rw, ac, o = raw[cc % 2], acc[cc % 2], ob[cc % 2]
                    fw.copy(POOL, rw[:, 0:3], halo3[:, cc, :])
                    for sb in range(NSB):
                        ps = nbank()
                        for c in range(16):
                            fw.mm(ps, wb[:, c, cc2 * 128:(cc2 + 1) * 128], hT3[:, c, sb * 512:(sb + 1) * 512],
                                  start=(c == 0), stop=(c == 15))
                        fw.copy(ACT, rw[:, 3 + sb * 512:3 + (sb + 1) * 512], ps, add=True)
                    fw.copy(POOL, halo3[:, cc, :], rw[:, TB:TB + 3], add=True)
                    fw.ts(DVE, ac, rw[:, 3:3 + TB], convw3[:, cc, 3:4], None, op0=ALU.mult)
                    for j in range(3):
                        fw.stt(ac, rw[:, j:j + TB], convw3[:, cc, j:j + 1], ac, ALU.mult, ALU.add)
                    fw.act(o, ac, AF.Silu)
                    fw.dma(STQ, qkT_sc[cc * 128:(cc + 1) * 128, t0:t0 + TB], o, add=True)
            fw.barrier()
            fw.release(mB)

            if stop == 'B1':
                fw.barrier()
                break
            ost = [fw.alloc(f"ost{i}", NTT * 256, BF16) for i in range(2)]
            for g in range(8):
                wb = wsx.get()
                o3 = r3(ost[g % 2], "p (t n) -> p t n", t=NTT)
                for tt in range(NTT):
                    ps = nbank()
                    for c in range(16):
                        fw.mm(ps[:, 0:256], hT3[:, c, tt * 128:(tt + 1) * 128], wb[:, c, :], start=(c == 0), stop=(c == 15))
                    fw.copy(ACT if tt % 2 == 0 else DVE, o3[:, tt, :], ps[:, 0:256], add=(tt > 0))
                for tt in range(NTT):
                    fw.dma(STQ, v_sc[t0 + tt * 128:t0 + (tt + 1) * 128, g * 256:(g + 1) * 256], o3[:, tt, :], add=True)

            if stop == 'B2':
                fw.barrier()
                break
            fw.barrier()
            fw.release(mB)
            ost = [fw.alloc(f"ost{i}", NTT * 256, BF16) for i in range(2)]
            s1 = [fw.alloc(f"s1{i}", 256, F32) for i in range(2)]
            s2 = [fw.alloc(f"s2{i}", 256, F32) for i in range(2)]
            k = 0
            for g in range(8):
                wo = wsx.get(ahead=2)
                wz = wsx.get(ahead=1)
                o3 = r3(ost[g % 2], "p (t n) -> p t n", t=NTT)
                for tt in range(NTT):
                    pso, psz = nbank(), nbank()
                    for c in range(16):
                        fw.mm(pso[:, 0:256], hT3[:, c, tt * 128:(tt + 1) * 128], wo[:, c, :], start=(c == 0), stop=(c == 15))
                    for c in range(16):
                        fw.mm(psz[:, 0:256], hT3[:, c, tt * 128:(tt + 1) * 128], wz[:, c, :], start=(c == 0), stop=(c == 15))
                    a1, a2 = s1[k % 2], s2[k % 2]
                    k += 1
                    fw.act(a1, pso[:, 0:256], AF.Sigmoid)
                    fw.act(a2, psz[:, 0:256], AF.Silu)
                    fw.tt(DVE, o3[:, tt, :], a1, a2, ALU.mult, add=(tt > 0))
                for tt in range(NTT):
                    fw.dma(STQ, oz_sc[t0 + tt * 128:t0 + (tt + 1) * 128, g * 256:(g + 1) * 256], o3[:, tt, :], add=True)

            if stop == 'B3':
                fw.barrier()
                break
            fw.barrier()
            fw.release(mB)
            wg = wsx.get()
            gst = fw.alloc("gst", NTT * 16, F32)
            gst3 = r3(gst, "p (t n) -> p t n", t=NTT)
            gtm = fw.alloc("gtm", NTT * 8, F32)
            gtm3 = r3(gtm, "p (t n) -> p t n", t=NTT)
            for tt in range(NTT):
                ps = nbank()
                for c in range(16):
                    fw.mm(ps[:, 0:16], hT3[:, c, tt * 128:(tt + 1) * 128], wg[:, c, :], start=(c == 0), stop=(c == 15))
                fw.tt(DVE, gst3[:, tt, :], ps[:, 0:16], gb_bc, ALU.add, add=(tt > 0))
            fw.act(gtm3, gst3[:, :, 8:16], AF.Exp, scale=-1.0)
            fw.act(gtm3, gtm3, AF.Ln, bias=1.0)
            fw.ts(DVE, gst3[:, :, 8:16], gtm3, -1.0, None, op0=ALU.mult, add=True)
            for tt in range(NTT):
                fw.dma(STQ, gates_sc[t0 + tt * 128:t0 + (tt + 1) * 128, :], gst3[:, tt, :], add=True)

            if stop == 'B4':
                fw.barrier()
                break
            fw.barrier()
            fw.release(mB)
            lraw = fw.alloc("lraw", 4 * 512, F32)
            lraw3 = r3(lraw, "p (c t) -> p c t", c=4)
            lsq = fw.alloc("lsq", 4 * 512, BF16)
            lsq3 = r3(lsq, "p (c t) -> p c t", c=4)
            ltmp = fw.alloc("ltmp", 512, F32)
            lrs = fw.alloc("lrs", 512, F32)
            lout = [fw.alloc(f"lout{i}", 4 * 512, BF16) for i in range(2)]
            k = 0
            for (gl, dst) in ((glq, cqT_sc), (glk, ckvT_sc)):
                wA = wsx.get(ahead=2)
                wB = wsx.get(ahead=1)
                for sb in range(NSB):
                    for c4 in range(4):
                        wb = wA if c4 < 2 else wB
                        ps = nbank()
                        for c in range(16):
                            fw.mm(ps, wb[:, c, (c4 % 2) * 128:(c4 % 2 + 1) * 128], hT3[:, c, sb * 512:(sb + 1) * 512],
                                  start=(c == 0), stop=(c == 15))
                        fw.copy(DVE, lraw3[:, c4, :], ps, add=(c4 > 0))
                        fw.act(lsq3[:, c4, :], lraw3[:, c4, :], AF.Square, add=(c4 > 0))
                    pss = nbank()
                    for c4 in range(4):
                        fw.mm(pss, onesb, lsq3[:, c4, :], start=(c4 == 0), stop=(c4 == 3))
                    fw.ts(DVE, ltmp, pss, 1.0 / 512, EPS, op0=ALU.mult, op1=ALU.add)
                    fw.act(ltmp, ltmp, AF.Ln)
                    fw.act(lrs, ltmp, AF.Exp, scale=-0.5)
                    lo = lout[k % 2]
                    k += 1
                    lo3 = r3(lo, "p (c t) -> p c t", c=4)
                    for c4 in range(4):
                        fw.stt(lo3[:, c4, :], lraw3[:, c4, :], gl[:, c4:c4 + 1], lrs, ALU.mult, ALU.mult, add=(c4 > 0))
                    for c4 in range(4):
                        fw.dma(STQ, dst[c4 * 128:(c4 + 1) * 128, t0 + sb * 512:t0 + (sb + 1) * 512], lo3[:, c4, :], add=True)
            if stop == 'B5a':
                fw.barrier()
                break
            wk = wsx.get()
            wkr = fw.alloc("wkr", 16 * 64, BF16)
            wkr3 = r3(wkr, "p (c n) -> p c n", c=16)
            fw.ts(POOL, wkr3[:, :, 0:32], wk[:, :, 32:64], -1.0, None, op0=ALU.mult)
            fw.copy(POOL, wkr3[:, :, 32:64], wk[:, :, 0:32], add=True)
            if stop == 'B5b':
                fw.barrier()
                break
            kst = [fw.alloc(f"kst{i}", 512, F32, parts=64) for i in range(2)]
            k = 0
            for sb in range(NSB):
                for (wsel, dst) in ((wk, kr_sc), (wkr3, krot_sc)):
                    ps = nbank()
                    for c in range(16):
                        fw.mm(ps[0:64, :], wsel[:, c, :], hT3[:, c, sb * 512:(sb + 1) * 512], start=(c == 0), stop=(c == 15))
                    ko = kst[k % 2]
                    k += 1
                    fw.copy(ACT, ko, ps[0:64, :])
                    fw.dma(STQ, dst[:, t0 + sb * 512:t0 + (sb + 1) * 512], ko, add=True)

            if stop == 'B5':
                fw.barrier()
                break
            fw.barrier()
            fw.release(mB)
            ob2 = [fw.alloc(f"ob2{i}", TB, BF16) for i in range(2)]
            k = 0
            for (func, dst) in ((AF.Silu, szbT_sc), (AF.Sigmoid, sgaT_sc), (AF.Sigmoid, sgbT_sc)):
                for g in range(8):
                    wb = wsx.get()
                    for cc2 in range(2):
                        cc = g * 2 + cc2
                        o = ob2[k % 2]
                        k += 1
                        for sb in range(NSB):
                            ps = nbank()
                            for c in range(16):
                                fw.mm(ps, wb[:, c, cc2 * 128:(cc2 + 1) * 128], hT3[:, c, sb * 512:(sb + 1) * 512],
                                      start=(c == 0), stop=(c == 15))
                            fw.act(o[:, sb * 512:(sb + 1) * 512], ps, func, add=(sb > 0))
                        fw.dma(STQ, dst[cc * 128:(cc + 1) * 128, t0:t0 + TB], o, add=True)
            fw.barrier()
        fw.release(par_mark)

        if stop in ('A', 'B', 'B1', 'B2', 'B3', 'B4', 'B5', 'B5a', 'B5b'):
            break
        Cm = fw.alloc("Cm", 8 * 260, F32)
        Cm3 = r3(Cm, "p (h e) -> p h e", h=8)
        Cb3 = [r3(fw.alloc(f"Cb{i}", 8 * 260, BF16), "p (h e) -> p h e", h=8) for i in range(2)]
        fw.memset(POOL, Cm, 0.0)
        fw.memset(POOL, Cb3[0], 0.0)
        fw.memset(POOL, Cb3[1], 0.0)
        qk3s = [r3(fw.alloc(f"qk{i}", 16 * 512, BF16), "p (c t) -> p c t", c=16) for i in range(2)]
        va4s = [r3(fw.alloc(f"va{i}", 4 * 8 * 258, BF16), "p (t h e) -> p t h e", t=4, h=8) for i in range(2)]
        oz3s = [r3(fw.alloc(f"ozt{i}", 4 * 2048, BF16), "p (t n) -> p t n", t=4) for i in range(2)]
        g3s = [r3(fw.alloc(f"gt{i}", 4 * 16, F32), "p (t n) -> p t n", t=4) for i in range(2)]
        uT3s = [r3(fw.alloc(f"uT{i}", 16 * 512, BF16), "p (c t) -> p c t", c=16) for i in range(2)]
        for i in range(2):
            fw.memset(POOL, va4s[i][:, :, :, 256:257], 1.0)
        Et = [fw.alloc(f"Et{i}", 32, F32) for i in range(2)]
        EXt = [fw.alloc(f"EXt{i}", 32, F32) for i in range(2)]
        sTs = [fw.alloc(f"sT{i}", 128, BF16) for i in range(3)]
        kws = [fw.alloc(f"kw{i}", 128, BF16) for i in range(3)]
        hb3s = [r3(fw.alloc(f"hbuf{i}", 8 * 256, F32), "p (h e) -> p h e", h=8) for i in range(2)]
        junks = [fw.alloc(f"junk{i}", 256, BF16) for i in range(8)]
        sss = [fw.alloc(f"ss{i}", 8, F32) for i in range(2)]
        dts = [fw.alloc(f"dt{i}", 64, F32) for i in range(2)]
        us = [fw.alloc(f"u{i}", 2048, BF16) for i in range(2)]
        for sc in range(NB5):
            tok0 = sc * 512
            qk3, v4, oz3, g3, uT3 = qk3s[sc % 2], va4s[sc % 2], oz3s[sc % 2], g3s[sc % 2], uT3s[sc % 2]
            fw.dma(SP, qk3, qkT_sc[:, tok0:tok0 + 512].v(lambda a: a.rearrange("(c p) t -> p c t", p=128)))
            for t in range(4):
                fw.dma(SP, v4[:, t, :, 0:256],
                       v_sc[tok0 + t * 128:tok0 + (t + 1) * 128, :].v(lambda a: a.rearrange("p (h e) -> p h e", h=8)), add=True)
            fw.dma(SP, oz3, oz_sc[tok0:tok0 + 512, :].v(lambda a: a.rearrange("(t p) n -> p t n", p=128)))
            fw.dma(SP, g3, gates_sc[tok0:tok0 + 512, :].v(lambda a: a.rearrange("(t p) n -> p t n", p=128)))
            for t in range(4):
                ci = sc * 4 + t
                Cbc, Cbn = Cb3[ci % 2], Cb3[(ci + 1) % 2]
                ic, lf = g3[:, t, 0:8], g3[:, t, 8:16]
                gps = banks[0]
                fw.mm(gps[:, 0:8], trif, lf, True, True)
                fw.mm(gps[:, 8:16], onesf, lf, True, True, excl=False)
                E, EX = Et[ci % 2], EXt[ci % 2]
                fw.tt(DVE, E[:, 0:8], ic, gps[:, 0:8], ALU.subtract)
                fw.tt(DVE, E[:, 8:16], E[:, 0:8], gps[:, 8:16], ALU.add, add=True)
                fw.ts(DVE, E[:, 16:24], gps[:, 0:8], LN_DK, None, op0=ALU.add, add=True)
                fw.copy(DVE, E[:, 24:32], gps[:, 8:16], add=True)
                fw.act(EX, E, AF.Exp)
                hb3, ss, dt, u = hb3s[ci % 2], sss[ci % 2], dts[ci % 2], us[ci % 2]
                dps = banks[1]
                for h in range(8):
                    tsl = slice(t * 128, (t + 1) * 128)
                    qTh, kTh = qk3[:, h, tsl], qk3[:, 8 + h, tsl]
                    sps = banks[2 + (h % 2)]
                    fw.mm(sps[:, 0:128], kTh, qTh)
                    sT = sTs[h % 3]
                    fw.stt(sT, sps[:, 0:128], EX[:, h:h + 1], trib, ALU.mult, ALU.mult)
                    kps = banks[7].v(lambda a: a.bitcast(BF16))
                    fw.transpose(kps[:, 0:128], kTh, ident)
                    kw = kws[h % 3]
                    fw.act(kw, kps[:, 0:128], AF.Copy, scale=EX[:, 8 + h:9 + h])
                    ops = banks[4 + ((h // 2) % 2)]
                    osl = slice((h % 2) * 256, (h % 2 + 1) * 256)
                    fw.mm(ops[:, osl], sT, v4[:, t, h, 0:256], True, False, excl=(h % 2 == 0))
                    fw.mm(ops[:, osl], qTh, Cbc[:, h, 0:256], False, True)
                    fw.mm(dps[:, h:h + 1], sT, onesb[:, 0:1], True, False, excl=(h == 0))
                    fw.mm(dps[:, h:h + 1], qTh, Cbc[:, h, 256:257], False, True)
                    ups = banks[6]
                    fw.mm(ups[:, 0:257], kw, v4[:, t, h, 0:257])
                    fw.stt(Cm3[:, h, 0:257], Cm3[:, h, 0:257], EX[:, 24 + h:25 + h], ups[:, 0:257], ALU.mult, ALU.add, add=True)
                    fw.copy(POOL, Cbn[:, h, 0:257], Cm3[:, h, 0:257], add=True)
                    fw.copy(ACT, hb3[:, h, :], ops[:, osl], add=True)
                    fw.act(junks[h], ops[:, osl], AF.Square, accum=ss[:, h:h + 1], accum_add=True)
                dec = EX[:, 16:24]
                fw.tt(DVE, dt[:, 0:8], dps[:, 0:8], dec, ALU.mult)
                fw.ts(DVE, dt[:, 8:16], dt[:, 0:8], -1.0, None, op0=ALU.mult)
                fw.tt(DVE, dt[:, 8:16], dt[:, 0:8], dt[:, 8:16], ALU.max)
                fw.ts(DVE, dt[:, 8:16], dt[:, 8:16], 1.0, None, op0=ALU.max)
                fw.recip(dt[:, 16:24], dt[:, 8:16])
                fw.tt(DVE, dt[:, 24:32], dt[:, 16:24], dec, ALU.mult)
                fw.tt(DVE, dt[:, 32:40], dt[:, 24:32], dt[:, 24:32], ALU.mult)
                fw.tt(DVE, dt[:, 40:48], ss, dt[:, 32:40], ALU.mult)
                fw.ts(DVE, dt[:, 40:48], dt[:, 40:48], 1.0 / 256, EPS, op0=ALU.mult, op1=ALU.add)
                fw.act(dt[:, 48:56], dt[:, 40:48], AF.Ln)
                fw.act(dt[:, 48:56], dt[:, 48:56], AF.Exp, scale=-0.5)
                fw.tt(DVE, dt[:, 56:64], dt[:, 24:32], dt[:, 48:56], ALU.mult)
                for h in range(8):
                    fw.stt(u[:, h * 256:(h + 1) * 256], hb3[:, h, :], dt[:, 56 + h:57 + h], oz3[:, t, h * 256:(h + 1) * 256],
                           ALU.mult, ALU.mult, add=(h > 0))
                for g in range(4):
                    pb = banks[2 + g % 2].v(lambda a: a.bitcast(BF16))
                    for j in range(4):
                        c = g * 4 + j
                        fw.transpose_add(pb[:, j * 128:(j + 1) * 128], u[:, c * 128:(c + 1) * 128], ident, sig=(j == 3))
                    for j in range(4):
                        c = g * 4 + j
                        if g % 2 == 0:
                            fw.act(uT3[:, c, t * 128:(t + 1) * 128], pb[:, j * 128:(j + 1) * 128], AF.Copy,
                                   scale=gmT[:, c:c + 1], add=True)
                        else:
                            fw.ts(DVE, uT3[:, c, t * 128:(t + 1) * 128], pb[:, j * 128:(j + 1) * 128], gmT[:, c:c + 1], None,
                                  op0=ALU.mult, add=True)
            for c in range(16):
                fw.dma(STQ, uaT_sc[c * 128:(c + 1) * 128, tok0:tok0 + 512], uT3[:, c, :], add=True)
        fw.barrier()
        fw.release(par_mark)

        if stop == 'C':
            break
        wuq = fw.alloc("wuq", 4 * 3072, BF16)
        wuq3 = r3(wuq, "p (c n) -> p c n", c=4)
        wuq4 = r3(wuq, "p (c h n) -> p c h n", c=4, h=16)
        wrot = fw.alloc("wrot", 4 * 1024, BF16)
        wrot4 = r3(wrot, "p (c h n) -> p c h n", c=4, h=16)
        wukv = fw.alloc("wukv", 4 * 4096, BF16)
        wukv3 = r3(wukv, "p (c n) -> p c n", c=4)
        wukv4 = r3(wukv, "p (c h n) -> p c h n", c=4, h=16)
        stg = [fw.alloc(f"stg{i}", 4 * 512, F32) for i in range(2)]
        k = 0
        for (wsrc, wdst, ncol) in ((Wl["w_uq"], wuq3, 3072), (Wl["w_ukv"], wukv3, 4096)):
            for j in range(ncol // 512):
                s_ = r3(stg[k % 2], "p (c n) -> p c n", c=4)
                k += 1
                fw.dma(SP, s_, wsrc[:, j * 512:(j + 1) * 512].v(lambda a: a.rearrange("(c p) n -> p c n", p=128)))
                fw.copy(POOL, wdst[:, :, j * 512:(j + 1) * 512], s_, add=(j > 0))
        for c in range(4):
            fw.ts(POOL, wrot4[:, c, :, 0:32], wuq4[:, c, :, 160:192], -1.0, None, op0=ALU.mult, add=(c > 0))
            fw.copy(POOL, wrot4[:, c, :, 32:64], wuq4[:, c, :, 128:160], add=True)
        cq3s = [r3(fw.alloc(f"cq{i}", 4 * 512, BF16), "p (c t) -> p c t", c=4) for i in range(2)]
        ckv3s = [r3(fw.alloc(f"ckv{i}", 4 * 512, BF16), "p (c t) -> p c t", c=4) for i in range(2)]
        krs = [fw.alloc(f"kr{i}", 512, F32, parts=64) for i in range(2)]
        krots = [fw.alloc(f"krot{i}", 512, F32, parts=64) for i in range(2)]
        coss = [fw.alloc(f"cos{i}", 512, F32, parts=64) for i in range(2)]
        sins = [fw.alloc(f"sin{i}", 512, F32, parts=64) for i in range(2)]
        kro = fw.alloc("kro", 512, F32, parts=64)
        rt1 = fw.alloc("rt1", 512, F32, parts=64)
        rt2 = fw.alloc("rt2", 512, F32, parts=64)
        sqkr = fw.alloc("sqkr", 512, BF16, parts=64)
        sqn = [fw.alloc(f"sqn{i}", 512, BF16) for i in range(2)]
        sqr = [fw.alloc(f"sqr{i}", 512, BF16, parts=64) for i in range(2)]
        lnt = [fw.alloc(f"lnt{i}", 512, F32) for i in range(2)]
        rsd = [fw.alloc(f"rsd{i}", 512, F32) for i in range(2)]
        onT = [fw.alloc(f"onT{i}", 512, BF16) for i in range(4)]
        orT = [fw.alloc(f"orT{i}", 512, BF16, parts=64) for i in range(4)]
        vo = [fw.alloc(f"vo{i}", 2048, BF16) for i in range(2)]
        gq, gk = gn["q"], gn["k"]
        kk = 0
        for b5 in range(NB5):
            tsl = slice(b5 * 512, (b5 + 1) * 512)
            cq3, ckv3 = cq3s[b5 % 2], ckv3s[b5 % 2]
            kr, krot, cs_, sn_ = krs[b5 % 2], krots[b5 % 2], coss[b5 % 2], sins[b5 % 2]
            fw.dma(SP, cq3, cqT_sc[:, tsl].v(lambda a: a.rearrange("(c p) t -> p c t", p=128)))
            fw.dma(SP, ckv3, ckvT_sc[:, tsl].v(lambda a: a.rearrange("(c p) t -> p c t", p=128)))
            fw.dma(SP, kr, kr_sc[:, tsl])
            fw.dma(SP, krot, krot_sc[:, tsl])
            fw.dma(SP, cs_, cos_sc[:, tsl])
            fw.dma(SP, sn_, sin_sc[:, tsl])
            fw.stt(rt1, kr, gk[0:64, 1:2], cs_, ALU.mult, ALU.mult)
            fw.stt(rt2, krot, gk[0:64, 2:3], sn_, ALU.mult, ALU.mult)
            fw.tt(DVE, kro, rt1, rt2, ALU.add)
            fw.act(sqkr, kr, AF.Square)
            for h in range(16):
                qn_ps, qr_ps, qx_ps = nbank(), nbank(), nbank()
                for c in range(4):
                    fw.mm(qn_ps, wuq3[:, c, h * 192:h * 192 + 128], cq3[:, c, :], start=(c == 0), stop=(c == 3))
                for c in range(4):
                    fw.mm(qr_ps[0:64, :], wuq3[:, c, h * 192 + 128:h * 192 + 192], cq3[:, c, :], start=(c == 0), stop=(c == 3))
                for c in range(4):
                    fw.mm(qx_ps[0:64, :], wrot4[:, c, h, :], cq3[:, c, :], start=(c == 0), stop=(c == 3))
                a_n, a_r, l_t, r_d = sqn[kk % 2], sqr[kk % 2], lnt[kk % 2], rsd[kk % 2]
                o_n, o_r = onT[kk % 4], orT[kk % 4]
                kk += 1
                fw.act(a_n, qn_ps, AF.Square)
                fw.act(a_r, qr_ps[0:64, :], AF.Square)
                ss_ps = nbank()
                fw.mm(ss_ps, onesb, a_n, True, False)
                fw.mm(ss_ps, onesb[0:64, :], a_r, False, True)
                fw.ts(DVE, l_t, ss_ps, 1.0 / 192, EPS, op0=ALU.mult, op1=ALU.add)
                fw.act(l_t, l_t, AF.Ln)
                fw.act(r_d, l_t, AF.Exp, scale=-0.5)
                fw.stt(o_n, qn_ps, gq[:, 0:1], r_d, ALU.mult, ALU.mult)
                fw.stt(rt1, qr_ps[0:64, :], gq[0:64, 1:2], cs_, ALU.mult, ALU.mult, deps=(a_r.buf,))
                fw.stt(rt2, qx_ps[0:64, :], gq[0:64, 2:3], sn_, ALU.mult, ALU.mult)
                fw.tt(DVE, rt1, rt1, rt2, ALU.add)
                fw.tt(DVE, o_r, rt1, r_d[0:64, :], ALU.mult)
                fw.dma(STQ, QT_sc[h, 0:128, tsl], o_n, add=True)
                fw.dma(STQ, QT_sc[h, 128:192, tsl], o_r, add=True)
                kn_ps = nbank()
                for c in range(4):
                    fw.mm(kn_ps, wukv3[:, c, h * 256:h * 256 + 128], ckv3[:, c, :], start=(c == 0), stop=(c == 3))
                a_n, l_t, r_d = sqn[kk % 2], lnt[kk % 2], rsd[kk % 2]
                o_n, o_r = onT[kk % 4], orT[kk % 4]
                kk += 1
                fw.act(a_n, kn_ps, AF.Square)
                ss_ps = nbank()
                fw.mm(ss_ps, onesb, a_n, True, False)
                fw.mm(ss_ps, onesb[0:64, :], sqkr, False, True)
                fw.ts(DVE, l_t, ss_ps, 1.0 / 192, EPS, op0=ALU.mult, op1=ALU.add)
                fw.act(l_t, l_t, AF.Ln)
                fw.act(r_d, l_t, AF.Exp, scale=-0.5)
                fw.stt(o_n, kn_ps, gk[:, 0:1], r_d, ALU.mult, ALU.mult)
                fw.tt(DVE, o_r, kro, r_d[0:64, :], ALU.mult)
                fw.dma(STQ, KT_sc[h, 0:128, tsl], o_n, add=True)
                fw.dma(STQ, KT_sc[h, 128:192, tsl], o_r, add=True)
            for t in range(4):
                vot = vo[t % 2]
                for g in range(4):
                    ps = nbank()
                    for c in range(4):
                        fw.mm(ps, ckv3[:, c, t * 128:(t + 1) * 128], wukv4[:, c, g * 4:(g + 1) * 4, 128:256],
                              start=(c == 0), stop=(c == 3))
                    fw.copy(ACT if g % 2 == 0 else DVE, vot[:, g * 512:(g + 1) * 512], ps, add=(g > 0))
                fw.dma(STQ, V_sc[b5 * 512 + t * 128:b5 * 512 + (t + 1) * 128, :], vot, add=True)
        fw.barrier()
        fw.release(par_mark)

        if stop == 'D':
            break
        Kn = [fw.alloc(f"Kn{i}", S, BF16) for i in range(2)]
        Kr = [fw.alloc(f"Kr{i}", S, BF16, parts=64) for i in range(2)]
        Vh3 = [r3(fw.alloc(f"Vh{i}", NT * 128, BF16), "p (t e) -> p t e", t=NT) for i in range(2)]
        Qn = [fw.alloc(f"Qn{i}", 512, BF16) for i in range(2)]
        Qr = [fw.alloc(f"Qr{i}", 512, BF16, parts=64) for i in range(2)]
        zb = [fw.alloc(f"zb{i}", 512, BF16) for i in range(2)]
        Pt = [fw.alloc(f"Pt{i}", 512, BF16) for i in range(4)]
        Dsb = [fw.alloc(f"Dsb{i}", 512, F32) for i in range(2)]
        rz = [fw.alloc(f"rz{i}", 512, F32) for i in range(2)]
        uo = [fw.alloc(f"uo{i}", 512, BF16) for i in range(2)]
        kc = 0

        def load_head(hh):
            fw.dma(SP, Kn[hh % 2], KT_sc[hh, 0:128, :])
            fw.dma(SP, Kr[hh % 2], KT_sc[hh, 128:192, :])
            fw.dma(SP, Vh3[hh % 2], V_sc[:, hh * 128:(hh + 1) * 128].v(lambda a: a.rearrange("(t p) e -> p t e", p=128)))

        def load_q(ii):
            hh, qq = divmod(ii, NB5)
            qs = slice(qq * 512, (qq + 1) * 512)
            fw.dma(SP, Qn[ii % 2], QT_sc[hh, 0:128, qs])
            fw.dma(SP, Qr[ii % 2], QT_sc[hh, 128:192, qs])
            fw.dma(SP, zb[ii % 2], szbT_sc[hh * 128:(hh + 1) * 128, qs])

        load_head(0)
        load_q(0)
        for h in range(16):
            Knh, Krh, V3 = Kn[h % 2], Kr[h % 2], Vh3[h % 2]
            if h + 1 < 16:
                load_head(h + 1)
            for qb in range(NB5):
                i = h * NB5 + qb
                qsl = slice(qb * 512, (qb + 1) * 512)
                Qni, Qri, zbi = Qn[i % 2], Qr[i % 2], zb[i % 2]
                if i + 1 < 16 * NB5:
                    load_q(i + 1)
                Ops, Dps = banks[4 + i % 2], banks[6 + i % 2]
                nk = 4 * qb + 4

                def smat(kt):
                    q0 = max(0, kt - 4 * qb) * 128
                    sp_ = banks[(kc + kt) % 4]
                    fw.mm(sp_[:, q0:512], Knh[:, kt * 128:(kt + 1) * 128], Qni[:, q0:512], True, False)
                    fw.mm(sp_[:, q0:512], Krh[0:64, kt * 128:(kt + 1) * 128], Qri[0:64, q0:512], False, True)

                smat(0)
                if nk > 1:
                    smat(1)
                for kt in range(nk):
                    if kt + 2 < nk:
                        smat(kt + 2)
                    r_ = kt - 4 * qb
                    q0 = max(0, r_) * 128
                    sp_ = banks[(kc + kt) % 4]
                    P = Pt[(kc + kt) % 4]
                    fw.act(P[:, q0:512], sp_[:, q0:512], AF.Exp, scale=ATT_SCALE)
                    if r_ >= 0:
                        fw.tt(POOL, P[:, q0:q0 + 128], P[:, q0:q0 + 128], trib, ALU.mult)
                    fw.mm(Ops[:, q0:512], V3[:, kt, :], P[:, q0:512], start=(kt == 0), stop=(kt == nk - 1))
                    fw.mm(Dps[:, q0:512], onesb, P[:, q0:512], start=(kt == 0), stop=(kt == nk - 1))
                kc += nk
                d_, z_, o_ = Dsb[i % 2], rz[i % 2], uo[i % 2]
                fw.copy(ACT, d_, Dps)
                fw.recip(d_, d_)
                fw.tt(DVE, z_, zbi, d_, ALU.mult)
                fw.tt(DVE, o_, Ops, z_, ALU.mult)
                fw.dma(STQ, ubT_sc[h * 128:(h + 1) * 128, qsl], o_, add=True)
        fw.barrier()
        fw.release(par_mark)

        if stop == 'E':
            break
        jobs = []
        for b5 in range(NB5):
            for g in range(8):
                jobs.append((Wl["w_a"], g * 256, 256))
                jobs.append((Wl["w_b"], g * 256, 256))
            for g in range(8):
                jobs.append((Wl["w_out"], g * 256, 256))
        wsx = WStream(fw, jobs, cast=[ACT])
        ua3s = [r3(fw.alloc(f"uaT{i}", 16 * 512, BF16), "p (c t) -> p c t", c=16) for i in range(1)]
        ub3s = [r3(fw.alloc(f"ubT{i}", 16 * 512, BF16), "p (c t) -> p c t", c=16) for i in range(1)]
        yT3 = r3(fw.alloc("yT", 16 * 512, BF16), "p (c t) -> p c t", c=16)
        xres = [fw.alloc(f"xres{i}", D, F32) for i in range(4)]
        sga = [fw.alloc(f"sga{i}", 512, BF16) for i in range(2)]
        sgb = [fw.alloc(f"sgb{i}", 512, BF16) for i in range(2)]
        ft1 = [fw.alloc(f"ft1{i}", 512, F32) for i in range(2)]
        ft2 = [fw.alloc(f"ft2{i}", 512, F32) for i in range(2)]
        kk = 0
        for b5 in range(NB5):
            tsl = slice(b5 * 512, (b5 + 1) * 512)
            ua3, ub3 = ua3s[0], ub3s[0]
            fw.dma(SP, ua3, uaT_sc[:, tsl].v(lambda a: a.rearrange("(c p) t -> p c t", p=128)))
            fw.dma(SP, ub3, ubT_sc[:, tsl].v(lambda a: a.rearrange("(c p) t -> p c t", p=128)))
            for g in range(8):
                wa = wsx.get(ahead=2)
                wb_ = wsx.get(ahead=1)
                for cc2 in range(2):
                    d = g * 2 + cc2
                    sa, sb_, f1, f2 = sga[kk % 2], sgb[kk % 2], ft1[kk % 2], ft2[kk % 2]
                    kk += 1
                    fw.dma(SP, sa, sgaT_sc[d * 128:(d + 1) * 128, tsl])
                    fw.dma(SP, sb_, sgbT_sc[d * 128:(d + 1) * 128, tsl])
                    pa, pb_ = nbank(), nbank()
                    for c in range(16):
                        fw.mm(pa, wa[:, c, cc2 * 128:(cc2 + 1) * 128], ua3[:, c, :], start=(c == 0), stop=(c == 15))
                    for c in range(16):
                        fw.mm(pb_, wb_[:, c, cc2 * 128:(cc2 + 1) * 128], ub3[:, c, :], start=(c == 0), stop=(c == 15))
                    fw.tt(DVE, f1, sa, pa, ALU.mult)
                    fw.tt(DVE, f2, sb_, pb_, ALU.mult)
                    fw.tt(DVE, yT3[:, d, :], f1, f2, ALU.add, add=(d > 0))
            for t in range(4):
                fw.dma(SP, xres[t], x_src[b5 * 512 + t * 128:b5 * 512 + (t + 1) * 128, :])
            for eg in range(8):
                wo = wsx.get()
                for t in range(4):
                    ps = nbank()
                    for d in range(16):
                        fw.mm(ps[:, 0:256], yT3[:, d, t * 128:(t + 1) * 128], wo[:, d, :], start=(d == 0), stop=(d == 15))
                    xs_ = xres[t][:, eg * 256:(eg + 1) * 256]
                    fw.tt(DVE, xs_, ps[:, 0:256], xs_, ALU.add, add=True)
            for t in range(4):
                fw.dma(STQ, x_dst[b5 * 512 + t * 128:b5 * 512 + (t + 1) * 128, :], xres[t], add=True)
        fw.barrier()
        fw.release(lay_mark)

    fw.finish()
    return nc, st, fw


_CACHE = {}


def _inv_freq():
    f = np.exp(-math.log(10000.0) * np.arange(0, 64, 2, dtype=np.float32) / np.float32(64)).astype(np.float32)
    return np.concatenate([f, f]).reshape(64, 1).astype(np.float32)


def run_layers(x, positions, weights, l0, l1):
    B, S, _ = x.shape
    NL = l1 - l0
    key = (S, NL)
    if key not in _CACHE:
        _CACHE[key] = build_program(S, NL)
    nc, st, fw = _CACHE[key]
    wsl = {n: np.ascontiguousarray(weights[n][l0:l1]) for n in W_NAMES}
    invf = _inv_freq()
    in_maps = []
    for b in range(B):
        m = {"x": np.ascontiguousarray(x[b]), "pos": np.ascontiguousarray(positions[b:b + 1]).astype(np.int32), "invf": invf}
        m.update(wsl)
        in_maps.append(m)
    res = run_bass_kernel_spmd(nc, in_maps, core_ids=list(range(B)))
    return np.stack([np.asarray(r["out"]) for r in res.results], axis=0)


LAYERS_PER_LAUNCH = 4


def kernel(x, positions, norm_g, w_in, gate_bias, conv_w, mlstm_norm_g, w_a, q_lat_g, kv_lat_g,
           w_uq, w_ukv, q_norm_g, k_norm_g, w_b, w_out):
    weights = {"norm_g": norm_g, "w_in": w_in, "gate_bias": gate_bias, "conv_w": conv_w, "mlstm_norm_g": mlstm_norm_g,
               "w_a": w_a, "q_lat_g": q_lat_g, "kv_lat_g": kv_lat_g, "w_uq": w_uq, "w_ukv": w_ukv,
               "q_norm_g": q_norm_g, "k_norm_g": k_norm_g, "w_b": w_b, "w_out": w_out}
    weights = {k: np.asarray(v, dtype=np.float32) for k, v in weights.items()}
    x = np.asarray(x, dtype=np.float32)
    positions = np.asarray(positions, dtype=np.int32)
    depth = w_in.shape[0]
    for l0 in range(0, depth, LAYERS_PER_LAUNCH):
        x = run_layers(x, positions, weights, l0, min(depth, l0 + LAYERS_PER_LAUNCH))
    return x.astype(np.float32)
```

```python
import numpy as np
import math
from contextlib import ExitStack
import concourse.bass as bass
import concourse.mybir as mybir
from concourse.bass_utils import run_bass_kernel_spmd

F32 = mybir.dt.float32
BF16 = mybir.dt.bfloat16
I32 = mybir.dt.int32
AF = mybir.ActivationFunctionType
ALU = mybir.AluOpType
AX = mybir.AxisListType


class Buf:
    __slots__ = ("w", "r", "w0", "name")

    def __init__(self, name=""):
        self.w = {}
        self.r = {}
        self.w0 = {}
        self.name = name


class T:
    __slots__ = ("ap", "buf")

    def __init__(self, ap, buf):
        self.ap = ap
        self.buf = buf

    def __getitem__(self, k):
        return T(self.ap[k], self.buf)

    def v(self, fn):
        return T(fn(self.ap), self.buf)


class Eng:
    def __init__(self, name, sem_id, dma_sem_ids):
        self.name = name
        self.sem = sem_id
        self.cnt = 0
        self.prog = []
        self.waited = {}
        self.dma_sems = dma_sem_ids
        self.dma_i = 0
        self.dma_val = {s: 0 for s in dma_sem_ids}
        self.pend = []
        self.ninstr = 0


class FW:
    NDMA = 8

    def __init__(self, nc, stack, arena_bytes):
        self.nc = nc
        self.sems = []
        def newsem(name):
            h = stack.enter_context(nc.semaphore(name))
            self.sems.append(h)
            return len(self.sems) - 1
        self.PE = Eng("PE", newsem("s_pe"), [])
        self.ACT = Eng("ACT", newsem("s_act"), [newsem(f"s_actq{i}") for i in range(self.NDMA)])
        self.DVE = Eng("DVE", newsem("s_dve"), [])
        self.POOL = Eng("POOL", newsem("s_pool"), [newsem(f"s_poolq{i}") for i in range(self.NDMA)])
        self.SP = Eng("SP", newsem("s_sp"), [newsem(f"s_spq{i}") for i in range(self.NDMA)])
        self.engs = [self.PE, self.ACT, self.DVE, self.POOL, self.SP]
        self.arena = stack.enter_context(nc.sbuf_tensor("arena", [128, arena_bytes // 4], F32))
        self.arena_bytes = arena_bytes
        self.top = 0
        self.banks = []
        for i in range(8):
            p = stack.enter_context(nc.psum_tensor(f"psb{i}", [128, 512], F32))
            self.banks.append(T(p[:, :], Buf(f"psb{i}")))

    def alloc(self, name, cols, dtype, parts=128):
        esz = 2 if dtype == BF16 else 4
        nbytes = (cols * esz + 31) // 32 * 32
        assert self.top + nbytes <= self.arena_bytes, f"arena overflow at {name}: {self.top}+{nbytes}"
        a = self.arena[0:parts, self.top // 4:(self.top + nbytes) // 4]
        if dtype != F32:
            a = a.bitcast(dtype)
        a = a[:, 0:cols]
        self.top += nbytes
        return T(a, Buf(name))

    def mark(self):
        return self.top

    def release(self, m):
        self.top = m

    def dram(self, name, shape, dtype, kind="Internal"):
        t = self.nc.dram_tensor(name, list(shape), dtype, kind=kind)
        return T(t.ap(), Buf(name))

    def op(self, E, fn, reads=(), writes=(), adds=(), sig=True, dma=False):
        waits = {}

        def need(d, skip_own=False):
            for sem, val in d.items():
                if skip_own and sem == E.sem:
                    continue
                if E.waited.get(sem, 0) < val:
                    if waits.get(sem, 0) < val:
                        waits[sem] = val

        pe = E is self.PE
        if not pe and self.PE.pend:
            for (prs, pws, pas) in self.PE.pend:
                for b in list(writes) + list(adds):
                    assert all(b is not x for x in prs), f"pending PE read hazard on {b.name}"
        for b in reads:
            need(b.w, skip_own=pe)
        for b in writes:
            need(b.w, skip_own=pe)
            need(b.r, skip_own=pe)
        for b in adds:
            need(b.r, skip_own=pe)
            need(b.w0, skip_own=pe)
        tok = None
        if dma:
            sem = E.dma_sems[E.dma_i]
            E.dma_i = (E.dma_i + 1) % len(E.dma_sems)
            prev = E.dma_val[sem]
            if prev > 0:
                need({sem: prev})
            E.dma_val[sem] = prev + 16
            tok = (sem, prev + 16)
        elif sig:
            E.cnt += 1
            tok = (E.sem, E.cnt)
        for sem, val in waits.items():
            E.waited[sem] = val
        wl = [(self.sems[s], v) for s, v in waits.items()]
        if dma:
            semh = self.sems[tok[0]]

            def run(e, wl=wl, fn=fn, semh=semh):
                for h, v in wl:
                    e.wait_ge(h, v)
                fn(e).then_inc(semh, 16)
        elif sig:
            semh = self.sems[E.sem]

            def run(e, wl=wl, fn=fn, semh=semh):
                for h, v in wl:
                    e.wait_ge(h, v)
                fn(e).then_inc(semh, 1)
        else:
            def run(e, wl=wl, fn=fn):
                for h, v in wl:
                    e.wait_ge(h, v)
                fn(e)
        E.prog.append(run)
        E.ninstr += 1
        if tok is None:
            E.pend.append((reads, writes, adds))
            return
        groups = [(reads, writes, adds)]
        if not dma and E.pend:
            groups += E.pend
            E.pend = []
        s, v = tok
        for (rs, ws, ads) in groups:
            for b in rs:
                if b.r.get(s, 0) < v:
                    b.r[s] = v
            for b in ws:
                b.w = {s: v}
                b.w0 = {s: v}
                b.r = {}
            for b in ads:
                if b.w.get(s, 0) < v:
                    b.w[s] = v

    def barrier(self):
        allt = {}
        for E in self.engs:
            if E.cnt > 0:
                allt[E.sem] = E.cnt
            for s, v in E.dma_val.items():
                if v > 0:
                    allt[s] = v
        for E in self.engs:
            assert not E.pend
            waits = {}
            for s, v in allt.items():
                if s == E.sem:
                    continue
                if E.waited.get(s, 0) < v:
                    waits[s] = v
                    E.waited[s] = v
            wl = [(self.sems[s], v) for s, v in waits.items()]
            if wl:
                def run(e, wl=wl):
                    for h, v in wl:
                        e.wait_ge(h, v)
                E.prog.append(run)

    def finish(self):
        self.barrier()
        nc = self.nc
        with nc.Block() as block:
            @block.tensor
            def _(e):
                for f in self.PE.prog:
                    f(e)

            @block.scalar
            def _(e):
                for f in self.ACT.prog:
                    f(e)

            @block.vector
            def _(e):
                for f in self.DVE.prog:
                    f(e)

            @block.gpsimd
            def _(e):
                for f in self.POOL.prog:
                    f(e)

            @block.sync
            def _(e):
                for f in self.SP.prog:
                    f(e)

    def mm(self, out, lhsT, rhs, start=True, stop=True, excl=None):
        o, l, r = out.ap, lhsT.ap, rhs.ap
        fn = lambda e: e.matmul(o, l, r, start=start, stop=stop)
        if excl is None:
            excl = start
        if excl:
            self.op(self.PE, fn, reads=(lhsT.buf, rhs.buf), writes=(out.buf,), sig=stop)
        else:
            self.op(self.PE, fn, reads=(lhsT.buf, rhs.buf), adds=(out.buf,), sig=stop)

    def transpose(self, out, in_, ident):
        o, i, d = out.ap, in_.ap, ident.ap
        self.op(self.PE, lambda e: e.transpose(o, i, d), reads=(in_.buf, ident.buf), writes=(out.buf,))

    def transpose_add(self, out, in_, ident, sig=True):
        o, i, d = out.ap, in_.ap, ident.ap
        self.op(self.PE, lambda e: e.transpose(o, i, d), reads=(in_.buf, ident.buf), adds=(out.buf,), sig=sig)

    def act(self, out, in_, func, bias=None, scale=None, accum=None, add=False, accum_add=False):
        o, i = out.ap, in_.ap
        kw = {}
        reads = [in_.buf]
        writes = [out.buf]
        if bias is not None:
            if isinstance(bias, T):
                kw["bias"] = bias.ap
                reads.append(bias.buf)
            else:
                kw["bias"] = bias
        if scale is not None:
            if isinstance(scale, T):
                kw["scale"] = scale.ap
                reads.append(scale.buf)
            else:
                kw["scale"] = scale
        adds = []
        if accum is not None:
            kw["accum_out"] = accum.ap
            (adds if accum_add else writes).append(accum.buf)
        if add:
            self.op(self.ACT, lambda e: e.activation(o, i, func, **kw), reads=reads, adds=writes + adds)
        else:
            self.op(self.ACT, lambda e: e.activation(o, i, func, **kw), reads=reads, writes=writes, adds=adds)

    def _e(self, E):
        return E

    def tt(self, E, out, in0, in1, op, add=False):
        o, a, b = out.ap, in0.ap, in1.ap
        fn = lambda e: e.tensor_tensor(out=o, in0=a, in1=b, op=op)
        if add:
            self.op(E, fn, reads=(in0.buf, in1.buf), adds=(out.buf,))
        else:
            self.op(E, fn, reads=(in0.buf, in1.buf), writes=(out.buf,))

    def ts(self, E, out, in0, s1, s2=None, op0=ALU.mult, op1=None, add=False):
        o, a = out.ap, in0.ap
        reads = [in0.buf]
        if isinstance(s1, T):
            reads.append(s1.buf)
            s1 = s1.ap
        if isinstance(s2, T):
            reads.append(s2.buf)
            s2 = s2.ap
        if op1 is None:
            fn = lambda e: e.tensor_scalar(out=o, in0=a, scalar1=s1, scalar2=None, op0=op0)
        else:
            fn = lambda e: e.tensor_scalar(out=o, in0=a, scalar1=s1, scalar2=s2, op0=op0, op1=op1)
        if add:
            self.op(E, fn, reads=reads, adds=(out.buf,))
        else:
            self.op(E, fn, reads=reads, writes=(out.buf,))

    def stt(self, out, in0, scalar, in1, op0, op1, add=False, deps=()):
        o, a, b = out.ap, in0.ap, in1.ap
        reads = [in0.buf, in1.buf] + list(deps)
        if isinstance(scalar, T):
            reads.append(scalar.buf)
            scalar = scalar.ap
        fn = lambda e: e.scalar_tensor_tensor(out=o, in0=a, scalar=scalar, in1=b, op0=op0, op1=op1)
        if add:
            self.op(self.DVE, fn, reads=reads, adds=(out.buf,))
        else:
            self.op(self.DVE, fn, reads=reads, writes=(out.buf,))

    def copy(self, E, out, in_, add=False):
        o, i = out.ap, in_.ap
        if E is self.ACT:
            fn = lambda e: e.copy(o, i)
        else:
            fn = lambda e: e.tensor_copy(out=o, in_=i)
        if add:
            self.op(E, fn, reads=(in_.buf,), adds=(out.buf,))
        else:
            self.op(E, fn, reads=(in_.buf,), writes=(out.buf,))

    def recip(self, out, in_):
        o, i = out.ap, in_.ap
        self.op(self.DVE, lambda e: e.reciprocal(o, i), reads=(in_.buf,), writes=(out.buf,))

    def memset(self, E, out, val, add=False):
        o = out.ap
        if add:
            self.op(E, lambda e: e.memset(o, val), adds=(out.buf,))
        else:
            self.op(E, lambda e: e.memset(o, val), writes=(out.buf,))

    def dma(self, E, out, in_, add=False, **kw):
        o, i = out.ap, in_.ap
        fn = lambda e: e.dma_start(out=o, in_=i, **kw)
        if add:
            self.op(E, fn, reads=(in_.buf,), adds=(out.buf,), dma=True)
        else:
            self.op(E, fn, reads=(in_.buf,), writes=(out.buf,), dma=True)


D = 2048
NIN = 15440
O_QA, O_KA, O_VA, O_OA, O_IA, O_FA, O_ZA = 0, 1024, 2048, 4096, 6144, 6152, 6160
O_CQ, O_CKV, O_KR, O_ZB, O_GA, O_GB = 8208, 8720, 9232, 9296, 11344, 13392
EPS = 1e-6
LN_DK = math.log(128.0 ** -0.5)
ATT_SCALE = 192.0 ** -0.5
TWO_PI_LO = 6.2831845
W_NAMES = ["norm_g", "w_in", "gate_bias", "conv_w", "mlstm_norm_g", "w_a", "q_lat_g", "kv_lat_g",
           "w_uq", "w_ukv", "q_norm_g", "k_norm_g", "w_b", "w_out"]
W_SHAPES = {"norm_g": [D], "w_in": [D, NIN], "gate_bias": [16], "conv_w": [4, 2048], "mlstm_norm_g": [2048],
            "w_a": [2048, D], "q_lat_g": [512], "kv_lat_g": [512], "w_uq": [512, 3072], "w_ukv": [512, 4096],
            "q_norm_g": [192], "k_norm_g": [192], "w_b": [2048, D], "w_out": [D, D]}


def r3(_x, pat, **kw):
    return _x.v(lambda a: a.rearrange(pat, **kw))


class WStream:
    def __init__(self, fw, jobs, nst=2, nbf=4, width=256, cast=None):
        self.fw = fw
        self.jobs = jobs
        self.cast = cast if cast is not None else [(fw.POOL, 0, 4), (fw.DVE, 4, 10), (fw.ACT, 10, 16)]
        self.st = [fw.alloc(f"wst{i}", 16 * width, F32) for i in range(nst)]
        self.bf = [fw.alloc(f"wbf{i}", 16 * width, BF16) for i in range(nbf)]
        self.issued = 0
        self.taken = 0
        self.tiles = {}

    def _issue(self):
        i = self.issued
        if i >= len(self.jobs):
            return
        src, col0, n = self.jobs[i]
        fw = self.fw
        ws = r3(self.st[i % len(self.st)][:, 0:16 * n], "p (c n) -> p c n", c=16)
        wb = r3(self.bf[i % len(self.bf)][:, 0:16 * n], "p (c n) -> p c n", c=16)
        fw.dma(fw.SP, ws, src[:, col0:col0 + n].v(lambda a: a.rearrange("(c p) n -> p c n", p=128)))
        for k, (eng, c0, c1) in enumerate(self.cast):
            fw.copy(eng, wb[:, c0:c1, :], ws[:, c0:c1, :], add=(k > 0))
        self.tiles[i] = wb
        self.issued += 1

    def get(self, ahead=1):
        while self.issued <= self.taken + ahead and self.issued < len(self.jobs):
            self._issue()
        t = self.tiles.pop(self.taken)
        self.taken += 1
        return t


def build_program(S, NL, debug=False, stop=None):
    nc = bass.Bass("TRN2", target_bir_lowering=False)
    st = ExitStack()
    fw = FW(nc, st, 188 * 1024)
    PE, ACT, DVE, POOL, SP = fw.PE, fw.ACT, fw.DVE, fw.POOL, fw.SP
    STQ = SP
    banks = fw.banks
    NT = S // 128
    TB = min(S, 2048)
    NTB = S // TB
    NTT = TB // 128
    NSB = TB // 512
    NB5 = S // 512
    skind = "ExternalOutput" if debug else "Internal"

    x_in = fw.dram("x", [S, D], F32, kind="ExternalInput")
    pos = fw.dram("pos", [1, S], I32, kind="ExternalInput")
    invf = fw.dram("invf", [64, 1], F32, kind="ExternalInput")
    Wd = {n: fw.dram(n, [NL] + W_SHAPES[n], F32, kind="ExternalInput") for n in W_NAMES}
    out_d = fw.dram("out", [S, D], F32, kind="ExternalOutput")
    xmid = [fw.dram(f"xmid{i}", [S, D], F32) for i in range(max(NL - 1, 0))]

    cos_sc = fw.dram("cos_sc", [64, S], F32, kind=skind)
    sin_sc = fw.dram("sin_sc", [64, S], F32, kind=skind)
    qkT_sc = fw.dram("qkT_sc", [2048, S], BF16, kind=skind)
    v_sc = fw.dram("v_sc", [S, 2048], BF16, kind=skind)
    oz_sc = fw.dram("oz_sc", [S, 2048], BF16, kind=skind)
    gates_sc = fw.dram("gates_sc", [S, 16], F32, kind=skind)
    cqT_sc = fw.dram("cqT_sc", [512, S], BF16, kind=skind)
    ckvT_sc = fw.dram("ckvT_sc", [512, S], BF16, kind=skind)
    kr_sc = fw.dram("kr_sc", [64, S], F32, kind=skind)
    krot_sc = fw.dram("krot_sc", [64, S], F32, kind=skind)
    szbT_sc = fw.dram("szbT_sc", [2048, S], BF16, kind=skind)
    sgaT_sc = fw.dram("sgaT_sc", [2048, S], BF16, kind=skind)
    sgbT_sc = fw.dram("sgbT_sc", [2048, S], BF16, kind=skind)
    uaT_sc = fw.dram("uaT_sc", [2048, S], BF16, kind=skind)
    QT_sc = fw.dram("QT_sc", [16, 192, S], BF16, kind=skind)
    KT_sc = fw.dram("KT_sc", [16, 192, S], BF16, kind=skind)
    V_sc = fw.dram("V_sc", [S, 2048], BF16, kind=skind)
    ubT_sc = fw.dram("ubT_sc", [2048, S], BF16, kind=skind)

    identf = fw.alloc("identf", 128, F32)
    ident = fw.alloc("ident", 128, BF16)
    trif = fw.alloc("trif", 128, F32)
    trib = fw.alloc("trib", 128, BF16)
    onesf = fw.alloc("onesf", 128, F32)
    onesb = fw.alloc("onesb", 128, BF16)
    fw.memset(POOL, identf, 0.0)
    ia = identf.ap
    fw.op(POOL, lambda e: e.affine_select(out=ia, in_=ia, pattern=[[-1, 128]], compare_op=ALU.not_equal,
                                           fill=1.0, base=0, channel_multiplier=1),
          reads=(identf.buf,), writes=(identf.buf,))
    fw.copy(DVE, ident, identf)
    fw.memset(POOL, onesf, 1.0)
    fw.copy(DVE, onesb, onesf)
    ta, oa = trif.ap, onesf.ap
    fw.op(POOL, lambda e: e.affine_select(out=ta, in_=oa, pattern=[[1, 128]], compare_op=ALU.is_ge,
                                           fill=0.0, base=0, channel_multiplier=-1),
          reads=(onesf.buf,), writes=(trif.buf,))
    fw.copy(DVE, trib, trif)

    pctr = [0]

    def nbank():
        b = banks[pctr[0] % 8]
        pctr[0] += 1
        return b

    m0 = fw.mark()
    CB = min(S, 1024)
    invt = fw.alloc("invt", 1, F32, parts=64)
    fw.dma(SP, invt, invf)
    posi = fw.alloc("posi", CB, I32, parts=64)
    ang = fw.alloc("ang", CB, F32, parts=64)
    tq = fw.alloc("tq", CB, F32, parts=64)
    ki = fw.alloc("ki", CB, I32, parts=64)
    kf = fw.alloc("kf", CB, F32, parts=64)
    rr_ = fw.alloc("rr_", CB, F32, parts=64)
    mk = fw.alloc("mk", CB, F32, parts=64)
    res = [fw.alloc(f"res{i}", CB, F32, parts=64) for i in range(2)]
    k = 0
    for cb in range(S // CB):
        sl = slice(cb * CB, (cb + 1) * CB)
        fw.dma(SP, posi, pos[:, sl].v(lambda a: a.partition_broadcast(64).rearrange("p o s -> p (o s)")))
        fw.copy(DVE, ang, posi)
        fw.ts(DVE, ang, ang, invt[:, 0:1], None, op0=ALU.mult)
        for (off, dst) in ((0.0, sin_sc), (0.25, cos_sc)):
            fw.ts(DVE, tq, ang, 1.0 / (2.0 * math.pi), off, op0=ALU.mult, op1=ALU.add)
            fw.copy(DVE, ki, tq)
            fw.copy(DVE, kf, ki)
            fw.tt(DVE, rr_, tq, kf, ALU.subtract)
            fw.ts(DVE, mk, rr_, 0.5, None, op0=ALU.is_gt)
            fw.tt(DVE, rr_, rr_, mk, ALU.subtract)
            fw.ts(DVE, mk, rr_, -0.5, None, op0=ALU.is_lt)
            fw.tt(DVE, rr_, rr_, mk, ALU.add)
            rs_ = res[k % 2]
            k += 1
            fw.act(rs_, rr_, AF.Sin, scale=TWO_PI_LO)
            fw.dma(SP, dst[:, sl], rs_, add=True)
    fw.barrier()
    fw.release(m0)

    for l in range(NL):
        if stop == 'R':
            break
        x_src = x_in if l == 0 else xmid[l - 1]
        x_dst = out_d if l == NL - 1 else xmid[l]
        Wl = {n: Wd[n][l] for n in W_NAMES}
        w_in = Wl["w_in"]
        lay_mark = fw.mark()

        g_bc = fw.alloc("g_bc", D, F32)
        fw.dma(SP, g_bc, Wl["norm_g"].v(lambda a: a.partition_broadcast(128)))
        convw = fw.alloc("convw", 64, F32)
        convw3 = r3(convw, "p (c j) -> p c j", c=16)
        for c in range(16):
            fw.dma(SP, convw3[:, c, :], Wl["conv_w"][:, c * 128:(c + 1) * 128].v(lambda a: a.rearrange("j p -> p j")),
                   add=(c > 0), allow_slow_non_contiguous=True)
        gmT = fw.alloc("gmT", 16, F32)
        for c4 in range(4):
            fw.dma(SP, gmT[:, c4 * 4:(c4 + 1) * 4],
                   Wl["mlstm_norm_g"][c4 * 512:(c4 + 1) * 512].v(lambda a: a.rearrange("(c p) -> p c", p=128)),
                   add=(c4 > 0), allow_slow_non_contiguous=True)
        glq = fw.alloc("glq", 4, F32)
        fw.dma(SP, glq, Wl["q_lat_g"].v(lambda a: a.rearrange("(c p) -> p c", p=128)), allow_slow_non_contiguous=True)
        glk = fw.alloc("glk", 4, F32)
        fw.dma(SP, glk, Wl["kv_lat_g"].v(lambda a: a.rearrange("(c p) -> p c", p=128)), allow_slow_non_contiguous=True)
        gn = {}
        colv = lambda a: a.rearrange("(p o) -> p o", o=1)
        for nm, src in (("q", Wl["q_norm_g"]), ("k", Wl["k_norm_g"])):
            t = fw.alloc("gn" + nm, 3, F32)
            fw.dma(SP, t[:, 0:1], src[0:128].v(colv), allow_slow_non_contiguous=True)
            fw.dma(SP, t[0:64, 1:2], src[128:192].v(colv), add=True, allow_slow_non_contiguous=True)
            fw.dma(SP, t[0:32, 2:3], src[160:192].v(colv), add=True, allow_slow_non_contiguous=True)
            fw.dma(SP, t[32:64, 2:3], src[128:160].v(colv), add=True, allow_slow_non_contiguous=True)
            gn[nm] = t
        gb_bc = fw.alloc("gb_bc", 16, F32)
        fw.dma(SP, gb_bc, Wl["gate_bias"].v(lambda a: a.partition_broadcast(128)))
        halo = fw.alloc("halo", 48, F32)
        halo3 = r3(halo, "p (c j) -> p c j", c=16)
        fw.memset(POOL, halo, 0.0)
        par_mark = fw.mark()
        if stop == 'P':
            break

        for tb in range(NTB):
            t0 = tb * TB
            fw.release(par_mark)
            hT = fw.alloc("hT", 16 * TB, BF16)
            hT3 = r3(hT, "p (c t) -> p c t", c=16)
            mA = fw.mark()
            xs = [fw.alloc(f"xs{i}", D, F32) for i in range(2)]
            xb = [fw.alloc(f"xb{i}", D, BF16) for i in range(2)]
            stat = [fw.alloc(f"stat{i}", 4, F32) for i in range(2)]
            for tt in range(NTT):
                xt, xbt, sv = xs[tt % 2], xb[tt % 2], stat[tt % 2]
                fw.dma(SP, xt, x_src[t0 + tt * 128:t0 + (tt + 1) * 128, :])
                fw.act(xbt, xt, AF.Square, accum=sv[:, 0:1])
                fw.ts(DVE, sv[:, 1:2], sv[:, 0:1], 1.0 / D, EPS, op0=ALU.mult, op1=ALU.add)
                fw.act(sv[:, 2:3], sv[:, 1:2], AF.Ln)
                fw.act(sv[:, 3:4], sv[:, 2:3], AF.Exp, scale=-0.5)
                fw.stt(xbt, xt, sv[:, 3:4], g_bc, ALU.mult, ALU.mult)
                for g in range(4):
                    pb = nbank().v(lambda a: a.bitcast(BF16))
                    for j in range(4):
                        c = g * 4 + j
                        fw.transpose_add(pb[:, j * 128:(j + 1) * 128], xbt[:, c * 128:(c + 1) * 128], ident, sig=(j == 3))
                    fw.copy(ACT if g % 2 == 0 else DVE, hT3[:, g * 4:(g + 1) * 4, tt * 128:(tt + 1) * 128],
                            r3(pb[:, 0:512], "p (c t) -> p c t", c=4), add=True)
            fw.barrier()
            fw.release(mA)
            if stop == 'A':
                break

            jobs = []
            for g in range(8):
                jobs.append((w_in, O_QA + g * 256, 256))
            for g in range(8):
                jobs.append((w_in, O_VA + g * 256, 256))
            for g in range(8):
                jobs.append((w_in, O_OA + g * 256, 256))
                jobs.append((w_in, O_ZA + g * 256, 256))
            jobs.append((w_in, O_IA, 16))
            jobs += [(w_in, O_CQ, 256), (w_in, O_CQ + 256, 256), (w_in, O_CKV, 256), (w_in, O_CKV + 256, 256)]
            jobs.append((w_in, O_KR, 64))
            for o in (O_ZB, O_GA, O_GB):
                for g in range(8):
                    jobs.append((w_in, o + g * 256, 256))
            wsx = WStream(fw, jobs)
            mB = fw.mark()

            raw = [fw.alloc(f"raw{i}", 3 + TB, F32) for i in range(2)]
            acc = [fw.alloc(f"acc{i}", TB, F32) for i in range(2)]
            ob = [fw.alloc(f"ob{i}", TB, BF16) for i in range(2)]
            for g in range(8):
                wb = wsx.get()
                for cc2 in range(2):
                    cc = g * 2 + cc2
                    rw, ac, o = raw[cc % 2], acc[cc % 2], ob[cc % 2]
                    fw.copy(POOL, rw[:, 0:3], halo3[:, cc, :])
                    for sb in range(NSB):
                        ps = nbank()
                        for c in range(16):
                            fw.mm(ps, wb[:, c, cc2 * 128:(cc2 + 1) * 128], hT3[:, c, sb * 512:(sb + 1) * 512],
                                  start=(c == 0), stop=(c == 15))
                        fw.copy(ACT, rw[:, 3 + sb * 512:3 + (sb + 1) * 512], ps, add=True)
                    fw.copy(POOL, halo3[:, cc, :], rw[:, TB:TB + 3], add=True)
                    fw.ts(DVE, ac, rw[:, 3:3 + TB], convw3[:, cc, 3:4], None, op0=ALU.mult)
                    for j in range(3):
                        fw.stt(ac, rw[:, j:j + TB], convw3[:, cc, j:j + 1], ac, ALU.mult, ALU.add)
                    fw.act(o, ac, AF.Silu)
                    fw.dma(STQ, qkT_sc[cc * 128:(cc + 1) * 128, t0:t0 + TB], o, add=True)
            fw.barrier()
            fw.release(mB)

            if stop == 'B1':
                fw.barrier()
                break
            ost = [fw.alloc(f"ost{i}", NTT * 256, BF16) for i in range(2)]
            for g in range(8):
                wb = wsx.get()
                o3 = r3(ost[g % 2], "p (t n) -> p t n", t=NTT)
                for tt in range(NTT):
                    ps = nbank()
                    for c in range(16):
                        fw.mm(ps[:, 0:256], hT3[:, c, tt * 128:(tt + 1) * 128], wb[:, c, :], start=(c == 0), stop=(c == 15))
                    fw.copy(ACT if tt % 2 == 0 else DVE, o3[:, tt, :], ps[:, 0:256], add=(tt > 0))
                for tt in range(NTT):
                    fw.dma(STQ, v_sc[t0 + tt * 128:t0 + (tt + 1) * 128, g * 256:(g + 1) * 256], o3[:, tt, :], add=True)

            if stop == 'B2':
                fw.barrier()
                break
            fw.barrier()
            fw.release(mB)
            ost = [fw.alloc(f"ost{i}", NTT * 256, BF16) for i in range(2)]
            s1 = [fw.alloc(f"s1{i}", 256, F32) for i in range(2)]
            s2 = [fw.alloc(f"s2{i}", 256, F32) for i in range(2)]
            k = 0
            for g in range(8):
                wo = wsx.get(ahead=2)
                wz = wsx.get(ahead=1)
                o3 = r3(ost[g % 2], "p (t n) -> p t n", t=NTT)
                for tt in range(NTT):
                    pso, psz = nbank(), nbank()
                    for c in range(16):
                        fw.mm(pso[:, 0:256], hT3[:, c, tt * 128:(tt + 1) * 128], wo[:, c, :], start=(c == 0), stop=(c == 15))
                    for c in range(16):
                        fw.mm(psz[:, 0:256], hT3[:, c, tt * 128:(tt + 1) * 128], wz[:, c, :], start=(c == 0), stop=(c == 15))
                    a1, a2 = s1[k % 2], s2[k % 2]
                    k += 1
                    fw.act(a1, pso[:, 0:256], AF.Sigmoid)
                    fw.act(a2, psz[:, 0:256], AF.Silu)
                    fw.tt(DVE, o3[:, tt, :], a1, a2, ALU.mult, add=(tt > 0))
                for tt in range(NTT):
                    fw.dma(STQ, oz_sc[t0 + tt * 128:t0 + (tt + 1) * 128, g * 256:(g + 1) * 256], o3[:, tt, :], add=True)

            if stop == 'B3':
                fw.barrier()
                break
            fw.barrier()
            fw.release(mB)
            wg = wsx.get()
            gst = fw.alloc("gst", NTT * 16, F32)
            gst3 = r3(gst, "p (t n) -> p t n", t=NTT)
            gtm = fw.alloc("gtm", NTT * 8, F32)
            gtm3 = r3(gtm, "p (t n) -> p t n", t=NTT)
            for tt in range(NTT):
                ps = nbank()
                for c in range(16):
                    fw.mm(ps[:, 0:16], hT3[:, c, tt * 128:(tt + 1) * 128], wg[:, c, :], start=(c == 0), stop=(c == 15))
                fw.tt(DVE, gst3[:, tt, :], ps[:, 0:16], gb_bc, ALU.add, add=(tt > 0))
            fw.act(gtm3, gst3[:, :, 8:16], AF.Exp, scale=-1.0)
            fw.act(gtm3, gtm3, AF.Ln, bias=1.0)
            fw.ts(DVE, gst3[:, :, 8:16], gtm3, -1.0, None, op0=ALU.mult, add=True)
            for tt in range(NTT):
                fw.dma(STQ, gates_sc[t0 + tt * 128:t0 + (tt + 1) * 128, :], gst3[:, tt, :], add=True)

            if stop == 'B4':
                fw.barrier()
                break
            fw.barrier()
            fw.release(mB)
            lraw = fw.alloc("lraw", 4 * 512, F32)
            lraw3 = r3(lraw, "p (c t) -> p c t", c=4)
            lsq = fw.alloc("lsq", 4 * 512, BF16)
            lsq3 = r3(lsq, "p (c t) -> p c t", c=4)
            ltmp = fw.alloc("ltmp", 512, F32)
            lrs = fw.alloc("lrs", 512, F32)
            lout = [fw.alloc(f"lout{i}", 4 * 512, BF16) for i in range(2)]
            k = 0
            for (gl, dst) in ((glq, cqT_sc), (glk, ckvT_sc)):
                wA = wsx.get(ahead=2)
                wB = wsx.get(ahead=1)
                for sb in range(NSB):
                    for c4 in range(4):
                        wb = wA if c4 < 2 else wB
                        ps = nbank()
                        for c in range(16):
                            fw.mm(ps, wb[:, c, (c4 % 2) * 128:(c4 % 2 + 1) * 128], hT3[:, c, sb * 512:(sb + 1) * 512],
                                  start=(c == 0), stop=(c == 15))
                        fw.copy(DVE, lraw3[:, c4, :], ps, add=(c4 > 0))
                        fw.act(lsq3[:, c4, :], lraw3[:, c4, :], AF.Square, add=(c4 > 0))
                    pss = nbank()
                    for c4 in range(4):
                        fw.mm(pss, onesb, lsq3[:, c4, :], start=(c4 == 0), stop=(c4 == 3))
                    fw.act(ltmp, pss, AF.Ln, scale=1.0 / 512, bias=EPS)
                    fw.act(lrs, ltmp, AF.Exp, scale=-0.5)
                    lo = lout[k % 2]
                    k += 1
                    lo3 = r3(lo, "p (c t) -> p c t", c=4)
                    for c4 in range(4):
                        fw.stt(lo3[:, c4, :], lraw3[:, c4, :], gl[:, c4:c4 + 1], lrs, ALU.mult, ALU.mult, add=(c4 > 0))
                    for c4 in range(4):
                        fw.dma(STQ, dst[c4 * 128:(c4 + 1) * 128, t0 + sb * 512:t0 + (sb + 1) * 512], lo3[:, c4, :], add=True)
            if stop == 'B5a':
                fw.barrier()
                break
            wk = wsx.get()
            wkr = fw.alloc("wkr", 16 * 64, BF16)
            wkr3 = r3(wkr, "p (c n) -> p c n", c=16)
            fw.ts(POOL, wkr3[:, :, 0:32], wk[:, :, 32:64], -1.0, None, op0=ALU.mult)
            fw.copy(POOL, wkr3[:, :, 32:64], wk[:, :, 0:32], add=True)
            if stop == 'B5b':
                fw.barrier()
                break
            kst = [fw.alloc(f"kst{i}", 512, F32, parts=64) for i in range(2)]
            k = 0
            for sb in range(NSB):
                for (wsel, dst) in ((wk, kr_sc), (wkr3, krot_sc)):
                    ps = nbank()
                    for c in range(16):
                        fw.mm(ps[0:64, :], wsel[:, c, :], hT3[:, c, sb * 512:(sb + 1) * 512], start=(c == 0), stop=(c == 15))
                    ko = kst[k % 2]
                    k += 1
                    fw.copy(ACT, ko, ps[0:64, :])
                    fw.dma(STQ, dst[:, t0 + sb * 512:t0 + (sb + 1) * 512], ko, add=True)

            if stop == 'B5':
                fw.barrier()
                break
            fw.barrier()
            fw.release(mB)
            ob2 = [fw.alloc(f"ob2{i}", TB, BF16) for i in range(2)]
            k = 0
            for (func, dst) in ((AF.Silu, szbT_sc), (AF.Sigmoid, sgaT_sc), (AF.Sigmoid, sgbT_sc)):
                for g in range(8):
                    wb = wsx.get()
                    for cc2 in range(2):
                        cc = g * 2 + cc2
                        o = ob2[k % 2]
                        k += 1
                        for sb in range(NSB):
                            ps = nbank()
                            for c in range(16):
                                fw.mm(ps, wb[:, c, cc2 * 128:(cc2 + 1) * 128], hT3[:, c, sb * 512:(sb + 1) * 512],
                                      start=(c == 0), stop=(c == 15))
                            fw.act(o[:, sb * 512:(sb + 1) * 512], ps, func, add=(sb > 0))
                        fw.dma(STQ, dst[cc * 128:(cc + 1) * 128, t0:t0 + TB], o, add=True)
            fw.barrier()
        fw.release(par_mark)

        if stop in ('A', 'B', 'B1', 'B2', 'B3', 'B4', 'B5', 'B5a', 'B5b'):
            break
        Cm = fw.alloc("Cm", 8 * 260, F32)
        Cm3 = r3(Cm, "p (h e) -> p h e", h=8)
        Cb3 = [r3(fw.alloc(f"Cb{i}", 8 * 260, BF16), "p (h e) -> p h e", h=8) for i in range(2)]
        fw.memset(POOL, Cm, 0.0)
        fw.memset(POOL, Cb3[0], 0.0)
        fw.memset(POOL, Cb3[1], 0.0)
        qk3s = [r3(fw.alloc(f"qk{i}", 16 * 512, BF16), "p (c t) -> p c t", c=16) for i in range(2)]
        va4s = [r3(fw.alloc(f"va{i}", 4 * 8 * 258, BF16), "p (t h e) -> p t h e", t=4, h=8) for i in range(2)]
        oz3s = [r3(fw.alloc(f"ozt{i}", 4 * 2048, BF16), "p (t n) -> p t n", t=4) for i in range(2)]
        g3s = [r3(fw.alloc(f"gt{i}", 4 * 16, F32), "p (t n) -> p t n", t=4) for i in range(2)]
        uT3s = [r3(fw.alloc(f"uT{i}", 16 * 512, BF16), "p (c t) -> p c t", c=16) for i in range(2)]
        for i in range(2):
            fw.memset(POOL, va4s[i][:, :, :, 256:257], 1.0)
        Et = [fw.alloc(f"Et{i}", 32, F32) for i in range(2)]
        EXt = [fw.alloc(f"EXt{i}", 32, F32) for i in range(2)]
        sTs = [fw.alloc(f"sT{i}", 128, BF16) for i in range(3)]
        kws = [fw.alloc(f"kw{i}", 128, BF16) for i in range(3)]
        hb3s = [r3(fw.alloc(f"hbuf{i}", 8 * 256, F32), "p (h e) -> p h e", h=8) for i in range(2)]
        junks = [fw.alloc(f"junk{i}", 256, BF16) for i in range(8)]
        sss = [fw.alloc(f"ss{i}", 8, F32) for i in range(2)]
        dts = [fw.alloc(f"dt{i}", 64, F32) for i in range(2)]
        us = [fw.alloc(f"u{i}", 2048, BF16) for i in range(2)]
        def c_loads(sc):
            tok0 = sc * 512
            qk3, v4, oz3, g3 = qk3s[sc % 2], va4s[sc % 2], oz3s[sc % 2], g3s[sc % 2]
            fw.dma(SP, qk3, qkT_sc[:, tok0:tok0 + 512].v(lambda a: a.rearrange("(c p) t -> p c t", p=128)))
            for t in range(4):
                fw.dma(SP, v4[:, t, :, 0:256],
                       v_sc[tok0 + t * 128:tok0 + (t + 1) * 128, :].v(lambda a: a.rearrange("p (h e) -> p h e", h=8)), add=True)
            fw.dma(SP, oz3, oz_sc[tok0:tok0 + 512, :].v(lambda a: a.rearrange("(t p) n -> p t n", p=128)))
            fw.dma(SP, g3, gates_sc[tok0:tok0 + 512, :].v(lambda a: a.rearrange("(t p) n -> p t n", p=128)))

        def c_head(sc, t):
            qk3, v4, g3 = qk3s[sc % 2], va4s[sc % 2], g3s[sc % 2]
            ci = sc * 4 + t
            Cbc, Cbn = Cb3[ci % 2], Cb3[(ci + 1) % 2]
            ic, lf = g3[:, t, 0:8], g3[:, t, 8:16]
            gps = banks[0]
            fw.mm(gps[:, 0:8], trif, lf, True, True)
            fw.mm(gps[:, 8:16], onesf, lf, True, True, excl=False)
            E, EX = Et[ci % 2], EXt[ci % 2]
            fw.tt(DVE, E[:, 0:8], ic, gps[:, 0:8], ALU.subtract)
            fw.tt(DVE, E[:, 8:16], E[:, 0:8], gps[:, 8:16], ALU.add, add=True)
            fw.ts(DVE, E[:, 16:24], gps[:, 0:8], LN_DK, None, op0=ALU.add, add=True)
            fw.copy(DVE, E[:, 24:32], gps[:, 8:16], add=True)
            fw.act(EX, E, AF.Exp)
            hb3, ss, dt = hb3s[ci % 2], sss[ci % 2], dts[ci % 2]
            dps = banks[1]
            for h in range(8):
                tsl = slice(t * 128, (t + 1) * 128)
                qTh, kTh = qk3[:, h, tsl], qk3[:, 8 + h, tsl]
                sps = banks[2 + (h % 2)]
                fw.mm(sps[:, 0:128], kTh, qTh)
                sT = sTs[h % 3]
                fw.stt(sT, sps[:, 0:128], EX[:, h:h + 1], trib, ALU.mult, ALU.mult)
                kps = banks[7].v(lambda a: a.bitcast(BF16))
                fw.transpose(kps[:, 0:128], kTh, ident)
                kw = kws[h % 3]
                fw.act(kw, kps[:, 0:128], AF.Copy, scale=EX[:, 8 + h:9 + h])
                ops = banks[4 + ((h // 2) % 2)]
                osl = slice((h % 2) * 256, (h % 2 + 1) * 256)
                fw.mm(ops[:, osl], sT, v4[:, t, h, 0:256], True, False, excl=(h % 2 == 0))
                fw.mm(ops[:, osl], qTh, Cbc[:, h, 0:256], False, True)
                fw.mm(dps[:, h:h + 1], sT, onesb[:, 0:1], True, False, excl=(h == 0))
                fw.mm(dps[:, h:h + 1], qTh, Cbc[:, h, 256:257], False, True)
                ups = banks[6]
                fw.mm(ups[:, 0:257], kw, v4[:, t, h, 0:257])
                fw.stt(Cm3[:, h, 0:257], Cm3[:, h, 0:257], EX[:, 24 + h:25 + h], ups[:, 0:257], ALU.mult, ALU.add, add=True)
                fw.copy(POOL, Cbn[:, h, 0:257], Cm3[:, h, 0:257], add=True)
                fw.copy(ACT, hb3[:, h, :], ops[:, osl], add=True)
                fw.act(junks[h], ops[:, osl], AF.Square, accum=ss[:, h:h + 1], accum_add=True)
            fw.tt(DVE, dt[:, 0:8], dps[:, 0:8], EX[:, 16:24], ALU.mult)

        def c_tail(sc, t):
            oz3, uT3 = oz3s[sc % 2], uT3s[sc % 2]
            ci = sc * 4 + t
            EX = EXt[ci % 2]
            hb3, ss, dt, u = hb3s[ci % 2], sss[ci % 2], dts[ci % 2], us[ci % 2]
            dec = EX[:, 16:24]
            fw.ts(DVE, dt[:, 8:16], dt[:, 0:8], -1.0, None, op0=ALU.mult)
            fw.tt(DVE, dt[:, 8:16], dt[:, 0:8], dt[:, 8:16], ALU.max)
            fw.ts(DVE, dt[:, 8:16], dt[:, 8:16], 1.0, None, op0=ALU.max)
            fw.recip(dt[:, 16:24], dt[:, 8:16])
            fw.tt(DVE, dt[:, 24:32], dt[:, 16:24], dec, ALU.mult)
            fw.tt(DVE, dt[:, 32:40], dt[:, 24:32], dt[:, 24:32], ALU.mult)
            fw.tt(DVE, dt[:, 40:48], ss, dt[:, 32:40], ALU.mult)
            fw.act(dt[:, 48:56], dt[:, 40:48], AF.Ln, scale=1.0 / 256, bias=EPS)
            fw.act(dt[:, 48:56], dt[:, 48:56], AF.Exp, scale=-0.5)
            fw.tt(DVE, dt[:, 56:64], dt[:, 24:32], dt[:, 48:56], ALU.mult)
            for h in range(8):
                fw.stt(u[:, h * 256:(h + 1) * 256], hb3[:, h, :], dt[:, 56 + h:57 + h], oz3[:, t, h * 256:(h + 1) * 256],
                       ALU.mult, ALU.mult, add=(h > 0))
            for g in range(4):
                pb = banks[2 + g % 2].v(lambda a: a.bitcast(BF16))
                for j in range(4):
                    c = g * 4 + j
                    fw.transpose_add(pb[:, j * 128:(j + 1) * 128], u[:, c * 128:(c + 1) * 128], ident, sig=(j == 3))
                for j in range(4):
                    c = g * 4 + j
                    if g % 2 == 0:
                        fw.act(uT3[:, c, t * 128:(t + 1) * 128], pb[:, j * 128:(j + 1) * 128], AF.Copy,
                               scale=gmT[:, c:c + 1], add=True)
                    else:
                        fw.ts(DVE, uT3[:, c, t * 128:(t + 1) * 128], pb[:, j * 128:(j + 1) * 128], gmT[:, c:c + 1], None,
                              op0=ALU.mult, add=True)
            if t == 3:
                tok0 = sc * 512
                for c in range(16):
                    fw.dma(STQ, uaT_sc[c * 128:(c + 1) * 128, tok0:tok0 + 512], uT3[:, c, :], add=True)

        prev = None
        for sc in range(NB5):
            c_loads(sc)
            for t in range(4):
                c_head(sc, t)
                if prev is not None:
                    c_tail(*prev)
                prev = (sc, t)
        c_tail(*prev)
        fw.barrier()
        fw.release(par_mark)

        if stop == 'C':
            break
        wuq = fw.alloc("wuq", 4 * 3072, BF16)
        wuq3 = r3(wuq, "p (c n) -> p c n", c=4)
        wuq4 = r3(wuq, "p (c h n) -> p c h n", c=4, h=16)
        wrot = fw.alloc("wrot", 4 * 1024, BF16)
        wrot4 = r3(wrot, "p (c h n) -> p c h n", c=4, h=16)
        wukv = fw.alloc("wukv", 4 * 4096, BF16)
        wukv3 = r3(wukv, "p (c n) -> p c n", c=4)
        wukv4 = r3(wukv, "p (c h n) -> p c h n", c=4, h=16)
        stg = [fw.alloc(f"stg{i}", 4 * 512, F32) for i in range(2)]
        k = 0
        for (wsrc, wdst, ncol) in ((Wl["w_uq"], wuq3, 3072), (Wl["w_ukv"], wukv3, 4096)):
            for j in range(ncol // 512):
                s_ = r3(stg[k % 2], "p (c n) -> p c n", c=4)
                k += 1
                fw.dma(SP, s_, wsrc[:, j * 512:(j + 1) * 512].v(lambda a: a.rearrange("(c p) n -> p c n", p=128)))
                fw.copy(POOL, wdst[:, :, j * 512:(j + 1) * 512], s_, add=(j > 0))
        for c in range(4):
            fw.ts(POOL, wrot4[:, c, :, 0:32], wuq4[:, c, :, 160:192], -1.0, None, op0=ALU.mult, add=(c > 0))
            fw.copy(POOL, wrot4[:, c, :, 32:64], wuq4[:, c, :, 128:160], add=True)
        cq3s = [r3(fw.alloc(f"cq{i}", 4 * 512, BF16), "p (c t) -> p c t", c=4) for i in range(2)]
        ckv3s = [r3(fw.alloc(f"ckv{i}", 4 * 512, BF16), "p (c t) -> p c t", c=4) for i in range(2)]
        krs = [fw.alloc(f"kr{i}", 512, F32, parts=64) for i in range(2)]
        krots = [fw.alloc(f"krot{i}", 512, F32, parts=64) for i in range(2)]
        coss = [fw.alloc(f"cos{i}", 512, F32, parts=64) for i in range(2)]
        sins = [fw.alloc(f"sin{i}", 512, F32, parts=64) for i in range(2)]
        kro = fw.alloc("kro", 512, F32, parts=64)
        rt1 = fw.alloc("rt1", 512, F32, parts=64)
        rt2 = fw.alloc("rt2", 512, F32, parts=64)
        sqkr = fw.alloc("sqkr", 512, BF16, parts=64)
        sqn = [fw.alloc(f"sqn{i}", 512, BF16) for i in range(2)]
        sqr = [fw.alloc(f"sqr{i}", 512, BF16, parts=64) for i in range(2)]
        lnt = [fw.alloc(f"lnt{i}", 512, F32) for i in range(2)]
        rsd = [fw.alloc(f"rsd{i}", 512, F32) for i in range(2)]
        onT = [fw.alloc(f"onT{i}", 512, BF16) for i in range(4)]
        orT = [fw.alloc(f"orT{i}", 512, BF16, parts=64) for i in range(4)]
        vo = [fw.alloc(f"vo{i}", 2048, BF16) for i in range(2)]
        gq, gk = gn["q"], gn["k"]
        kk = 0
        for b5 in range(NB5):
            tsl = slice(b5 * 512, (b5 + 1) * 512)
            cq3, ckv3 = cq3s[b5 % 2], ckv3s[b5 % 2]
            kr, krot, cs_, sn_ = krs[b5 % 2], krots[b5 % 2], coss[b5 % 2], sins[b5 % 2]
            fw.dma(SP, cq3, cqT_sc[:, tsl].v(lambda a: a.rearrange("(c p) t -> p c t", p=128)))
            fw.dma(SP, ckv3, ckvT_sc[:, tsl].v(lambda a: a.rearrange("(c p) t -> p c t", p=128)))
            fw.dma(SP, kr, kr_sc[:, tsl])
            fw.dma(SP, krot, krot_sc[:, tsl])
            fw.dma(SP, cs_, cos_sc[:, tsl])
            fw.dma(SP, sn_, sin_sc[:, tsl])
            fw.stt(rt1, kr, gk[0:64, 1:2], cs_, ALU.mult, ALU.mult)
            fw.stt(rt2, krot, gk[0:64, 2:3], sn_, ALU.mult, ALU.mult)
            fw.tt(DVE, kro, rt1, rt2, ALU.add)
            fw.act(sqkr, kr, AF.Square)
            for h in range(16):
                qn_ps, qr_ps, qx_ps = nbank(), nbank(), nbank()
                for c in range(4):
                    fw.mm(qn_ps, wuq3[:, c, h * 192:h * 192 + 128], cq3[:, c, :], start=(c == 0), stop=(c == 3))
                for c in range(4):
                    fw.mm(qr_ps[0:64, :], wuq3[:, c, h * 192 + 128:h * 192 + 192], cq3[:, c, :], start=(c == 0), stop=(c == 3))
                for c in range(4):
                    fw.mm(qx_ps[0:64, :], wrot4[:, c, h, :], cq3[:, c, :], start=(c == 0), stop=(c == 3))
                a_n, a_r, l_t, r_d = sqn[kk % 2], sqr[kk % 2], lnt[kk % 2], rsd[kk % 2]
                o_n, o_r = onT[kk % 4], orT[kk % 4]
                kk += 1
                fw.act(a_n, qn_ps, AF.Square)
                fw.act(a_r, qr_ps[0:64, :], AF.Square)
                ss_ps = nbank()
                fw.mm(ss_ps, onesb, a_n, True, False)
                fw.mm(ss_ps, onesb[0:64, :], a_r, False, True)
                fw.act(l_t, ss_ps, AF.Ln, scale=1.0 / 192, bias=EPS)
                fw.act(r_d, l_t, AF.Exp, scale=-0.5)
                fw.stt(o_n, qn_ps, gq[:, 0:1], r_d, ALU.mult, ALU.mult)
                fw.stt(rt1, qr_ps[0:64, :], gq[0:64, 1:2], cs_, ALU.mult, ALU.mult, deps=(a_r.buf,))
                fw.stt(rt2, qx_ps[0:64, :], gq[0:64, 2:3], sn_, ALU.mult, ALU.mult)
                fw.tt(DVE, rt1, rt1, rt2, ALU.add)
                fw.tt(DVE, o_r, rt1, r_d[0:64, :], ALU.mult)
                fw.dma(STQ, QT_sc[h, 0:128, tsl], o_n, add=True)
                fw.dma(STQ, QT_sc[h, 128:192, tsl], o_r, add=True)
                kn_ps = nbank()
                for c in range(4):
                    fw.mm(kn_ps, wukv3[:, c, h * 256:h * 256 + 128], ckv3[:, c, :], start=(c == 0), stop=(c == 3))
                a_n, l_t, r_d = sqn[kk % 2], lnt[kk % 2], rsd[kk % 2]
                o_n, o_r = onT[kk % 4], orT[kk % 4]
                kk += 1
                fw.act(a_n, kn_ps, AF.Square)
                ss_ps = nbank()
                fw.mm(ss_ps, onesb, a_n, True, False)
                fw.mm(ss_ps, onesb[0:64, :], sqkr, False, True)
                fw.act(l_t, ss_ps, AF.Ln, scale=1.0 / 192, bias=EPS)
                fw.act(r_d, l_t, AF.Exp, scale=-0.5)
                fw.stt(o_n, kn_ps, gk[:, 0:1], r_d, ALU.mult, ALU.mult)
                fw.tt(DVE, o_r, kro, r_d[0:64, :], ALU.mult)
                fw.dma(STQ, KT_sc[h, 0:128, tsl], o_n, add=True)
                fw.dma(STQ, KT_sc[h, 128:192, tsl], o_r, add=True)
            for t in range(4):
                vot = vo[t % 2]
                for g in range(4):
                    ps = nbank()
                    for c in range(4):
                        fw.mm(ps, ckv3[:, c, t * 128:(t + 1) * 128], wukv4[:, c, g * 4:(g + 1) * 4, 128:256],
                              start=(c == 0), stop=(c == 3))
                    fw.copy(ACT if g % 2 == 0 else DVE, vot[:, g * 512:(g + 1) * 512], ps, add=(g > 0))
                fw.dma(STQ, V_sc[b5 * 512 + t * 128:b5 * 512 + (t + 1) * 128, :], vot, add=True)
        fw.barrier()
        fw.release(par_mark)

        if stop == 'D':
            break
        Kn = [fw.alloc(f"Kn{i}", S, BF16) for i in range(2)]
        Kr = [fw.alloc(f"Kr{i}", S, BF16, parts=64) for i in range(2)]
        Vh3 = [r3(fw.alloc(f"Vh{i}", NT * 128, BF16), "p (t e) -> p t e", t=NT) for i in range(2)]
        Qn = [fw.alloc(f"Qn{i}", 512, BF16) for i in range(2)]
        Qr = [fw.alloc(f"Qr{i}", 512, BF16, parts=64) for i in range(2)]
        zb = [fw.alloc(f"zb{i}", 512, BF16) for i in range(2)]
        Pt = [fw.alloc(f"Pt{i}", 512, BF16) for i in range(4)]
        Dsb = [fw.alloc(f"Dsb{i}", 512, F32) for i in range(2)]
        rz = [fw.alloc(f"rz{i}", 512, F32) for i in range(2)]
        uo = [fw.alloc(f"uo{i}", 512, BF16) for i in range(2)]
        kc = 0

        def load_head(hh):
            fw.dma(SP, Kn[hh % 2], KT_sc[hh, 0:128, :])
            fw.dma(SP, Kr[hh % 2], KT_sc[hh, 128:192, :])
            fw.dma(SP, Vh3[hh % 2], V_sc[:, hh * 128:(hh + 1) * 128].v(lambda a: a.rearrange("(t p) e -> p t e", p=128)))

        def load_q(ii):
            hh, qq = divmod(ii, NB5)
            qs = slice(qq * 512, (qq + 1) * 512)
            fw.dma(SP, Qn[ii % 2], QT_sc[hh, 0:128, qs])
            fw.dma(SP, Qr[ii % 2], QT_sc[hh, 128:192, qs])
            fw.dma(SP, zb[ii % 2], szbT_sc[hh * 128:(hh + 1) * 128, qs])

        load_head(0)
        load_q(0)
        for h in range(16):
            Knh, Krh, V3 = Kn[h % 2], Kr[h % 2], Vh3[h % 2]
            if h + 1 < 16:
                load_head(h + 1)
            for qb in range(NB5):
                i = h * NB5 + qb
                qsl = slice(qb * 512, (qb + 1) * 512)
                Qni, Qri, zbi = Qn[i % 2], Qr[i % 2], zb[i % 2]
                if i + 1 < 16 * NB5:
                    load_q(i + 1)
                Ops, Dps = banks[4 + i % 2], banks[6 + i % 2]
                nk = 4 * qb + 4

                def smat(kt):
                    q0 = max(0, kt - 4 * qb) * 128
                    sp_ = banks[(kc + kt) % 4]
                    fw.mm(sp_[:, q0:512], Knh[:, kt * 128:(kt + 1) * 128], Qni[:, q0:512], True, False)
                    fw.mm(sp_[:, q0:512], Krh[0:64, kt * 128:(kt + 1) * 128], Qri[0:64, q0:512], False, True)

                smat(0)
                if nk > 1:
                    smat(1)
                for kt in range(nk):
                    if kt + 2 < nk:
                        smat(kt + 2)
                    r_ = kt - 4 * qb
                    q0 = max(0, r_) * 128
                    sp_ = banks[(kc + kt) % 4]
                    P = Pt[(kc + kt) % 4]
                    fw.act(P[:, q0:512], sp_[:, q0:512], AF.Exp, scale=ATT_SCALE)
                    if r_ >= 0:
                        fw.tt(POOL, P[:, q0:q0 + 128], P[:, q0:q0 + 128], trib, ALU.mult)
                    fw.mm(Ops[:, q0:512], V3[:, kt, :], P[:, q0:512], start=(kt == 0), stop=(kt == nk - 1))
                    fw.mm(Dps[:, q0:512], onesb, P[:, q0:512], start=(kt == 0), stop=(kt == nk - 1))
                kc += nk
                d_, z_, o_ = Dsb[i % 2], rz[i % 2], uo[i % 2]
                fw.copy(ACT, d_, Dps)
                fw.recip(d_, d_)
                fw.tt(DVE, z_, zbi, d_, ALU.mult)
                fw.tt(DVE, o_, Ops, z_, ALU.mult)
                fw.dma(STQ, ubT_sc[h * 128:(h + 1) * 128, qsl], o_, add=True)
        fw.barrier()
        fw.release(par_mark)

        if stop == 'E':
            break
        jobs = []
        for b5 in range(NB5):
            for g in range(8):
                jobs.append((Wl["w_a"], g * 256, 256))
                jobs.append((Wl["w_b"], g * 256, 256))
            for g in range(8):
                jobs.append((Wl["w_out"], g * 256, 256))
        wsx = WStream(fw, jobs, cast=[(POOL, 0, 3), (ACT, 3, 10), (DVE, 10, 16)])
        ua3s = [r3(fw.alloc(f"uaT{i}", 16 * 512, BF16), "p (c t) -> p c t", c=16) for i in range(1)]
        ub3s = [r3(fw.alloc(f"ubT{i}", 16 * 512, BF16), "p (c t) -> p c t", c=16) for i in range(1)]
        yT3 = r3(fw.alloc("yT", 16 * 512, BF16), "p (c t) -> p c t", c=16)
        xres = [fw.alloc(f"xres{i}", D, F32) for i in range(4)]
        sga = [fw.alloc(f"sga{i}", 512, BF16) for i in range(2)]
        sgb = [fw.alloc(f"sgb{i}", 512, BF16) for i in range(2)]
        ft1 = [fw.alloc(f"ft1{i}", 512, F32) for i in range(2)]
        ft2 = [fw.alloc(f"ft2{i}", 512, F32) for i in range(2)]
        kk = 0
        for b5 in range(NB5):
            tsl = slice(b5 * 512, (b5 + 1) * 512)
            ua3, ub3 = ua3s[0], ub3s[0]
            fw.dma(SP, ua3, uaT_sc[:, tsl].v(lambda a: a.rearrange("(c p) t -> p c t", p=128)))
            fw.dma(SP, ub3, ubT_sc[:, tsl].v(lambda a: a.rearrange("(c p) t -> p c t", p=128)))
            for g in range(8):
                wa = wsx.get(ahead=2)
                wb_ = wsx.get(ahead=1)
                for cc2 in range(2):
                    d = g * 2 + cc2
                    sa, sb_, f1, f2 = sga[kk % 2], sgb[kk % 2], ft1[kk % 2], ft2[kk % 2]
                    kk += 1
                    fw.dma(SP, sa, sgaT_sc[d * 128:(d + 1) * 128, tsl])
                    fw.dma(SP, sb_, sgbT_sc[d * 128:(d + 1) * 128, tsl])
                    pa, pb_ = nbank(), nbank()
                    for c in range(16):
                        fw.mm(pa, wa[:, c, cc2 * 128:(cc2 + 1) * 128], ua3[:, c, :], start=(c == 0), stop=(c == 15))
                    for c in range(16):
                        fw.mm(pb_, wb_[:, c, cc2 * 128:(cc2 + 1) * 128], ub3[:, c, :], start=(c == 0), stop=(c == 15))
                    fw.tt(DVE, f1, sa, pa, ALU.mult)
                    fw.tt(DVE, f2, sb_, pb_, ALU.mult)
                    fw.tt(DVE, yT3[:, d, :], f1, f2, ALU.add, add=(d > 0))
            for t in range(4):
                fw.dma(SP, xres[t], x_src[b5 * 512 + t * 128:b5 * 512 + (t + 1) * 128, :])
            for eg in range(8):
                wo = wsx.get()
                for t in range(4):
                    ps = nbank()
                    for d in range(16):
                        fw.mm(ps[:, 0:256], yT3[:, d, t * 128:(t + 1) * 128], wo[:, d, :], start=(d == 0), stop=(d == 15))
                    xs_ = xres[t][:, eg * 256:(eg + 1) * 256]
                    fw.tt(DVE, xs_, ps[:, 0:256], xs_, ALU.add, add=True)
            for t in range(4):
                fw.dma(STQ, x_dst[b5 * 512 + t * 128:b5 * 512 + (t + 1) * 128, :], xres[t], add=True)
        fw.barrier()
        fw.release(lay_mark)

    fw.finish()
    return nc, st, fw


_CACHE = {}


def _inv_freq():
    f = np.exp(-math.log(10000.0) * np.arange(0, 64, 2, dtype=np.float32) / np.float32(64)).astype(np.float32)
    return np.concatenate([f, f]).reshape(64, 1).astype(np.float32)


def run_layers(x, positions, weights, l0, l1):
    B, S, _ = x.shape
    NL = l1 - l0
    key = (S, NL)
    if key not in _CACHE:
        _CACHE[key] = build_program(S, NL)
    nc, st, fw = _CACHE[key]
    wsl = {n: np.ascontiguousarray(weights[n][l0:l1]) for n in W_NAMES}
    invf = _inv_freq()
    in_maps = []
    for b in range(B):
        m = {"x": np.ascontiguousarray(x[b]), "pos": np.ascontiguousarray(positions[b:b + 1]).astype(np.int32), "invf": invf}
        m.update(wsl)
        in_maps.append(m)
    res = run_bass_kernel_spmd(nc, in_maps, core_ids=list(range(B)))
    return np.stack([np.asarray(r["out"]) for r in res.results], axis=0)


LAYERS_PER_LAUNCH = 4


def kernel(x, positions, norm_g, w_in, gate_bias, conv_w, mlstm_norm_g, w_a, q_lat_g, kv_lat_g,
           w_uq, w_ukv, q_norm_g, k_norm_g, w_b, w_out):
    weights = {"norm_g": norm_g, "w_in": w_in, "gate_bias": gate_bias, "conv_w": conv_w, "mlstm_norm_g": mlstm_norm_g,
               "w_a": w_a, "q_lat_g": q_lat_g, "kv_lat_g": kv_lat_g, "w_uq": w_uq, "w_ukv": w_ukv,
               "q_norm_g": q_norm_g, "k_norm_g": k_norm_g, "w_b": w_b, "w_out": w_out}
    weights = {k: np.asarray(v, dtype=np.float32) for k, v in weights.items()}
    x = np.asarray(x, dtype=np.float32)
    positions = np.asarray(positions, dtype=np.int32)
    depth = w_in.shape[0]
    for l0 in range(0, depth, LAYERS_PER_LAUNCH):
        x = run_layers(x, positions, weights, l0, min(depth, l0 + LAYERS_PER_LAUNCH))
    return x.astype(np.float32)
```
